# Optimizing a Trainium2 kernel written in Bass

```python
import jax, jax.numpy as jnp
from jax import lax
import numpy as np

D_MODEL = 1024
BATCH = 8
SEQ = 4096
DEPTH = 2

N_MIXERS = 2
ATTN_PAIRS = ((128, 1), (512, 4), (2048, 16))
N_ATTN_GROUPS = len(ATTN_PAIRS)
HEADS_PER_GROUP = 8
HEAD_DIM = D_MODEL // HEADS_PER_GROUP
ATTN_WIDTH = HEADS_PER_GROUP * HEAD_DIM
QKV_WIDTH = N_ATTN_GROUPS * 3 * ATTN_WIDTH
N_ALIBI_HEADS = N_ATTN_GROUPS * HEADS_PER_GROUP
Q_BLOCK = 128
POOL_WINDOWS = (2, 4, 8, 16)
POOL_GROUPS = len(POOL_WINDOWS)
POOL_GROUP_DIM = D_MODEL // POOL_GROUPS
D_FF = ((8 * D_MODEL + 3 * 256 - 1) // (3 * 256)) * 256
N_ATTN_LAYERS = (DEPTH + 1) // 2
N_POOL_LAYERS = DEPTH // 2
RMS_EPS = 1e-6

kernel_name = "dilated_attn_pool_hybrid_trunk"


def rmsnorm(x, g):
    xf = x.astype(jnp.float32)
    y = xf * lax.rsqrt(jnp.mean(xf * xf, axis=-1, keepdims=True) + RMS_EPS)
    return (y * g.astype(jnp.float32)).astype(x.dtype)


def alibi_slopes():
    n = N_ALIBI_HEADS
    return jnp.exp2(-8.0 * jnp.arange(1, n + 1, dtype=jnp.float32) / n)


def dilated_window_attention(q, k, v, window, dil, slopes):
    B, S, H, D = q.shape
    L = S // dil
    w_sub = window // dil
    nb = -(-L // Q_BLOCK)
    Lp = nb * Q_BLOCK
    pad = Lp - L
    Bd = B * dil

    def fold(a):
        return a.reshape(B, L, dil, H, D).transpose(0, 2, 1, 3, 4).reshape(Bd, L, H, D)

    def band(a):
        a = jnp.pad(a, ((0, 0), (Q_BLOCK, pad), (0, 0), (0, 0))).reshape(Bd, nb + 1, Q_BLOCK, H, D)
        return jnp.concatenate([a[:, :-1], a[:, 1:]], axis=2)

    qb = jnp.pad(fold(q), ((0, 0), (0, pad), (0, 0), (0, 0))).reshape(Bd, nb, Q_BLOCK, H, D)
    kb = band(fold(k))
    vb = band(fold(v))

    s = jnp.einsum('bnqhd,bnkhd->bnhqk', qb, kb) * (D ** -0.5)
    qi = jnp.arange(Q_BLOCK)[:, None]
    ki = jnp.arange(2 * Q_BLOCK)[None, :]
    delta = Q_BLOCK + qi - ki
    blk = jnp.arange(nb)[:, None, None]
    key_idx = blk * Q_BLOCK + ki[None] - Q_BLOCK
    valid = (delta >= 0)[None] & (delta <= w_sub)[None] & (key_idx >= 0)
    bias = -slopes[:, None, None] * (delta * dil).astype(jnp.float32)[None]
    s = s + bias[None, None]
    s = jnp.where(valid[None, :, None], s, -jnp.inf)
    m = jnp.max(s, axis=-1, keepdims=True)
    p = jnp.exp(s - m)
    den = jnp.sum(p, axis=-1, keepdims=True)
    o = jnp.einsum('bnhqk,bnkhd->bnqhd', p / den, vb)
    lse = (m + jnp.log(den))[..., 0].transpose(0, 1, 3, 2)

    o = o.reshape(Bd, Lp, H, D)[:, :L]
    o = o.reshape(B, dil, L, H, D).transpose(0, 2, 1, 3, 4).reshape(B, S, H, D)
    lse = lse.reshape(Bd, Lp, H)[:, :L]
    lse = lse.reshape(B, dil, L, H).transpose(0, 2, 1, 3).reshape(B, S, H)
    return o, lse


def dilated_attention_mixer(h, w_qkv, w_out):
    B, S, _ = h.shape
    qkv = (h @ w_qkv).astype(jnp.float32).reshape(B, S, N_ATTN_GROUPS, 3, HEADS_PER_GROUP, HEAD_DIM)
    slopes = alibi_slopes().reshape(N_ATTN_GROUPS, HEADS_PER_GROUP)
    outs, lses = [], []
    for g, (window, dil) in enumerate(ATTN_PAIRS):
        o, lse = dilated_window_attention(qkv[:, :, g, 0], qkv[:, :, g, 1], qkv[:, :, g, 2], window, dil, slopes[g])
        outs.append(o)
        lses.append(lse)
    o = jnp.stack(outs, axis=0)
    wts = jax.nn.softmax(jnp.stack(lses, axis=0), axis=0)
    o = jnp.sum(wts[..., None] * o, axis=0).reshape(B, S, ATTN_WIDTH)
    return o.astype(h.dtype) @ w_out


def trailing_mean(u, w):
    B, S, C = u.shape
    c = jnp.cumsum(u, axis=1)
    shifted = jnp.concatenate([jnp.zeros((B, w, C), u.dtype), c[:, :S - w]], axis=1)
    count = jnp.minimum(jnp.arange(1, S + 1), w).astype(jnp.float32)[None, :, None]
    return (c - shifted) / count


def pooling_mixer(h, w_in, w_group, scale):
    B, S, _ = h.shape
    u = (h @ w_in).astype(jnp.float32).reshape(B, S, POOL_GROUPS, POOL_GROUP_DIM)
    ys = [trailing_mean(u[:, :, g], w) - u[:, :, g] for g, w in enumerate(POOL_WINDOWS)]
    y = jnp.stack(ys, axis=2)
    y = jnp.einsum('bsgc,gcd->bsgd', y, w_group.astype(jnp.float32)).reshape(B, S, D_MODEL)
    return (y * scale.astype(jnp.float32)).astype(h.dtype)


def swiglu_ffn(h, w_gate_up, w_down):
    gu = h @ w_gate_up
    gate, up = gu[..., :D_FF], gu[..., D_FF:]
    return (jax.nn.silu(gate) * up) @ w_down


def setup_inputs(seed: int = 0) -> dict:
    key = jax.random.key(seed)
    ks = jax.random.split(key, 13)
    f32 = jnp.float32
    nA, nP = N_ATTN_LAYERS, N_POOL_LAYERS
    return {
        "x": jax.random.normal(ks[0], (BATCH, SEQ, D_MODEL), f32),
        "attn_norm": 1.0 + 0.02 * jax.random.normal(ks[1], (nA, D_MODEL), f32),
        "w_qkv": jax.random.normal(ks[2], (nA, D_MODEL, QKV_WIDTH), f32) * D_MODEL ** -0.5,
        "w_attn_out": jax.random.normal(ks[3], (nA, ATTN_WIDTH, D_MODEL), f32) * ATTN_WIDTH ** -0.5,
        "pool_norm": 1.0 + 0.02 * jax.random.normal(ks[4], (nP, D_MODEL), f32),
        "w_pool_in": jax.random.normal(ks[5], (nP, D_MODEL, D_MODEL), f32) * D_MODEL ** -0.5,
        "w_pool_group": jax.random.normal(ks[6], (nP, POOL_GROUPS, POOL_GROUP_DIM, POOL_GROUP_DIM), f32) * POOL_GROUP_DIM ** -0.5,
        "pool_scale": 0.5 + 0.1 * jax.random.normal(ks[7], (nP, D_MODEL), f32),
        "ffn_norm": 1.0 + 0.02 * jax.random.normal(ks[8], (DEPTH, D_MODEL), f32),
        "w_ffn_gate_up": jax.random.normal(ks[9], (DEPTH, D_MODEL, 2 * D_FF), f32) * D_MODEL ** -0.5,
        "w_ffn_down": jax.random.normal(ks[10], (DEPTH, D_FF, D_MODEL), f32) * D_FF ** -0.5,
        "final_norm": 1.0 + 0.02 * jax.random.normal(ks[11], (D_MODEL,), f32),
    }


def reference(x, attn_norm, w_qkv, w_attn_out, pool_norm, w_pool_in, w_pool_group, pool_scale,
              ffn_norm, w_ffn_gate_up, w_ffn_down, final_norm):
    for i in range(DEPTH):
        j = i // N_MIXERS
        if i % N_MIXERS == 0:
            x = x + dilated_attention_mixer(rmsnorm(x, attn_norm[j]), w_qkv[j], w_attn_out[j])
        else:
            x = x + pooling_mixer(rmsnorm(x, pool_norm[j]), w_pool_in[j], w_pool_group[j], pool_scale[j])
        x = x + swiglu_ffn(rmsnorm(x, ffn_norm[i]), w_ffn_gate_up[i], w_ffn_down[i])
    return rmsnorm(x, final_norm)
```

```python
import contextlib
import math
import numpy as np
import ml_dtypes
import concourse.bass as bass
import concourse.mybir as mybir
from concourse.bass_utils import run_bass_kernel_spmd

F32 = mybir.dt.float32
BF16 = mybir.dt.bfloat16
AF = mybir.ActivationFunctionType
ALU = mybir.AluOpType

ENGS = ("pe", "act", "dve", "pool", "sp")
DMA_RING = 12

S_TOK = 4096
D = 1024
DFF = 2816
NF = DFF // 128
QKVW = 9216
PAIRS = ((128, 1), (512, 4), (2048, 16))
EPS = 1e-6
NEG = -1.0e5
DBG = dict(iters=24, attn_blocks=True, proj=True)


class Sched:
    def __init__(self, nc):
        self.nc = nc
        self.ops = {e: [] for e in ENGS}
        self.last_w = {}
        self.readers = {}
        self.ndma = {e: 0 for e in ENGS}

    def _deps(self, eng, r, w):
        deps = set()
        for res in r:
            t = self.last_w.get(res)
            if t is not None:
                deps.add((t, "raw"))
        for res in w:
            t = self.last_w.get(res)
            if t is not None:
                deps.add((t, "waw"))
            for t in self.readers.get(res, ()):
                deps.add((t, "war"))
        out = set()
        for t, kind in deps:
            if t[0] == "eng" and t[1] == eng:
                if eng in ("pe", "sp") or kind != "raw":
                    continue
            out.add(t)
        return out

    def _commit(self, tok, r, w):
        for res in r:
            self.readers.setdefault(res, []).append(tok)
        for res in w:
            self.last_w[res] = tok
            self.readers[res] = []

    def op(self, eng, fn, r=(), w=()):
        idx = len(self.ops[eng])
        deps = self._deps(eng, r, w)
        tok = ("eng", eng, idx)
        self.ops[eng].append(dict(fn=fn, deps=deps, dma=None, sig=False))
        self._commit(tok, r, w)
        return tok

    def dma(self, eng, fn, r=(), w=()):
        deps = self._deps(eng, r, w)
        n = self.ndma[eng]
        self.ndma[eng] += 1
        tok = ("dma", eng, n)
        if n >= DMA_RING:
            deps.add(("dma", eng, n - DMA_RING))
        self.ops[eng].append(dict(fn=fn, deps=deps, dma=n, sig=False))
        self._commit(tok, r, w)
        return tok

    def _tail_tokens(self):
        toks = []
        for e in ENGS:
            for i in range(len(self.ops[e]) - 1, -1, -1):
                o = self.ops[e][i]
                if o["dma"] is None and o["fn"] is not None:
                    toks.append(("eng", e, i))
                    break
            n = self.ndma[e]
            for k in range(max(0, n - DMA_RING), n):
                toks.append(("dma", e, k))
        return toks

    def barrier(self):
        toks = self._tail_tokens()
        self.last_w = {}
        self.readers = {}
        for e in ENGS:
            deps = set(t for t in toks if not (t[0] == "eng" and t[1] == e and e in ("pe", "sp")))
            self.ops[e].append(dict(fn=None, deps=deps, dma=None, sig=False))

    def finish(self, eng="sp"):
        deps = set(t for t in self._tail_tokens() if t[0] == "dma")
        self.ops[eng].append(dict(fn=None, deps=deps, dma=None, sig=False))

    def emit(self):
        nc = self.nc
        for e in ENGS:
            for o in self.ops[e]:
                for t in o["deps"]:
                    if t[0] == "eng":
                        self.ops[t[1]][t[2]]["sig"] = True
        cnt = {}
        for e in ENGS:
            c = 0
            for i, o in enumerate(self.ops[e]):
                if o["sig"]:
                    assert o["fn"] is not None
                    c += 1
                cnt[(e, i)] = c
        with contextlib.ExitStack() as st:
            esem = {e: st.enter_context(nc.semaphore("prog_" + e)) for e in ENGS}
            dsem = {}
            for e in ENGS:
                if self.ndma[e]:
                    dsem[e] = [st.enter_context(nc.semaphore("dq_%s_%d" % (e, k)))
                               for k in range(min(DMA_RING, self.ndma[e]))]
            block = st.enter_context(nc.Block())
            ops = self.ops

            def run(e, eng):
                seen = {}
                for o in ops[e]:
                    need = {}
                    for t in o["deps"]:
                        if t[0] == "eng":
                            key = ("eng", t[1])
                            val = cnt[(t[1], t[2])]
                            sem = esem[t[1]]
                        else:
                            key = ("dma", t[1], t[2] % DMA_RING)
                            val = 16 * (t[2] // DMA_RING + 1)
                            sem = dsem[t[1]][t[2] % DMA_RING]
                        if seen.get(key, 0) >= val:
                            continue
                        if key not in need or need[key][1] < val:
                            need[key] = (sem, val)
                    for key, (sem, val) in need.items():
                        eng.wait_ge(sem, val)
                        seen[key] = val
                    if o["fn"] is None:
                        continue
                    ins = o["fn"](eng)
                    if o["dma"] is not None:
                        ins.then_inc(dsem[e][o["dma"] % DMA_RING], 16)
                    elif o["sig"]:
                        ins.then_inc(esem[e], 1)

            @block.tensor
            def _(eng):
                run("pe", eng)

            @block.scalar
            def _(eng):
                run("act", eng)

            @block.vector
            def _(eng):
                run("dve", eng)

            @block.gpsimd
            def _(eng):
                run("pool", eng)

            @block.sync
            def _(eng):
                run("sp", eng)


class Arena:
    def __init__(self, big):
        self.big = big
        self.cap = big.shape[1]
        self.top = 0

    def alloc(self, shape, dt):
        n = 1
        for s in shape[1:]:
            n *= s
        e16 = n * (2 if dt == F32 else 1)
        off = self.top
        self.top += (e16 + 15) // 16 * 16
        assert self.top <= self.cap, "SBUF arena overflow: %d > %d" % (self.top, self.cap)
        ap = self.big[0:shape[0], off:off + e16]
        if dt == F32:
            ap = ap.bitcast(F32)
        if len(shape) == 3:
            ap = ap.rearrange("p (a b) -> p a b", a=shape[1])
        elif len(shape) == 4:
            ap = ap.rearrange("p (a b c) -> p a b c", a=shape[1], b=shape[2])
        return ap


def build_nc(stop_after=None):
    nc = bass.Bass("TRN2", target_bir_lowering=False)

    def din(name, shape, dt=F32):
        return nc.dram_tensor(name, list(shape), dt, kind="ExternalInput").ap()

    x = din("x", [S_TOK, D])
    attn_norm = din("attn_norm", [D])
    w_qkv = din("w_qkv", [D, QKVW])
    w_attn_out = din("w_attn_out", [D, D])
    pool_norm = din("pool_norm", [D])
    w_pool_in = din("w_pool_in", [D, D])
    w_pool_group = din("w_pool_group", [4, 256, 256])
    pool_scale = din("pool_scale", [D])
    ffn_norm = din("ffn_norm", [2, D])
    w_gu = din("w_ffn_gate_up", [2, D, 2 * DFF])
    w_down = din("w_ffn_down", [2, DFF, D])
    final_norm = din("final_norm", [D])
    c_ident = din("c_ident", [128, 128], BF16)
    c_base = din("c_base", [128, 256])
    c_rc = din("c_rc", [4, 16])
    out = nc.dram_tensor("out", [S_TOK, D], F32, kind="ExternalOutput").ap()

    xs = [nc.dram_tensor("xs%d" % i, [S_TOK, D], F32).ap() for i in range(4)]
    OTs = nc.dram_tensor("ots", [D, S_TOK], BF16).ap()

    st = contextlib.ExitStack()
    with st:
        big = st.enter_context(nc.sbuf_tensor("big", [128, 105984], BF16))
        ps = [st.enter_context(nc.psum_tensor("ps%d" % i, [128, 512], F32)) for i in range(8)]
        S = Sched(nc)
        A = Arena(big)

        def tiles(ap):
            return ap.rearrange("(n p) d -> n p d", p=128)

        idt = A.alloc([128, 128], BF16)
        ones = A.alloc([128, 128], BF16)
        base2 = A.alloc([128, 512], F32)
        S.dma("sp", lambda e: e.dma_start(out=idt, in_=c_ident), w=["idt"])
        S.dma("sp", lambda e: e.dma_start(out=base2[:, 0:256], in_=c_base), w=["base"])
        S.dma("sp", lambda e: e.dma_start(out=base2[:, 256:512], in_=c_base), w=["base"])
        S.op("dve", lambda e: e.memset(ones, 1.0), w=["ones"])
        const_top = A.top

        def pT(b):
            return ps[b][:].bitcast(BF16)

        def sq_accum(xt_ap, junk, ssq_col, rx, wss):
            S.op("act", lambda e: e.activation(out=junk, in_=xt_ap, func=AF.Square, accum_out=ssq_col),
                 r=rx, w=["junk"] + wss)

        def rstd_from(ssq_ap, rst_ap, rr, ww):
            S.op("dve", lambda e: e.tensor_scalar(out=rst_ap, in0=ssq_ap, scalar1=1.0 / D, scalar2=EPS,
                                                 op0=ALU.mult, op1=ALU.add), r=rr, w=ww)
            S.op("act", lambda e: e.activation(out=rst_ap, in_=rst_ap, func=AF.Sqrt), r=ww, w=ww)
            S.op("dve", lambda e: e.reciprocal(out=rst_ap, in_=rst_ap), r=ww, w=ww)

        def normed_transposed(xt_ap, rst_col, gB, hn_ap, bank, dst_ap, rx, rrst, hn_res, dst_res):
            S.op("dve", lambda e: e.scalar_tensor_tensor(out=hn_ap, in0=xt_ap, scalar=rst_col, in1=gB,
                                                        op0=ALU.mult, op1=ALU.mult),
                 r=rx + rrst + ["gB"], w=[hn_res])

            def tr(e):
                for kc in range(8):
                    ins = e.transpose(out=pT(bank)[:, kc * 128:(kc + 1) * 128],
                                      in_=hn_ap[:, kc * 128:(kc + 1) * 128], identity=idt)
                return ins
            S.op("pe", tr, r=[hn_res, "idt"], w=["ps%d" % bank])
            S.op("act", lambda e: e.activation(out=dst_ap,
                                               in_=pT(bank).rearrange("p (a b) -> p a b", a=8),
                                               func=AF.Copy),
                 r=["ps%d" % bank], w=[dst_res])

        def phase_attention(xin, xout):
            hT = A.alloc([128, 8, S_TOK], BF16)
            gB = A.alloc([128, D], F32)
            ssq = A.alloc([128, 32], F32)
            rst = A.alloc([128, 32], F32)
            junk = A.alloc([128, D], BF16)
            p0_top = A.top
            xt = [A.alloc([128, D], F32) for _ in range(3)]
            hn = [A.alloc([128, D], BF16) for _ in range(2)]
            S.dma("sp", lambda e: e.dma_start(out=gB, in_=attn_norm.partition_broadcast(128)), w=["gB"])
            S.op("dve", lambda e: e.memset(ssq, 0.0), w=["ssq"])
            xin_t = tiles(xin)
            for tt in range(32):
                s3 = tt % 3
                S.dma("sp", lambda e, tt=tt, s3=s3: e.dma_start(out=xt[s3], in_=xin_t[tt]), w=["xt%d" % s3])
                sq_accum(xt[s3], junk, ssq[:, tt:tt + 1], ["xt%d" % s3, "ssq"], ["ssq"])
            rstd_from(ssq, rst, ["ssq"], ["rst"])
            for tt in range(32):
                s3, s2 = (tt + 32) % 3, tt % 2
                S.dma("sp", lambda e, tt=tt, s3=s3: e.dma_start(out=xt[s3], in_=xin_t[tt]), w=["xt%d" % s3])
                normed_transposed(xt[s3], rst[:, tt:tt + 1], gB, hn[s2], tt % 2,
                                  hT[:, :, tt * 128:(tt + 1) * 128],
                                  ["xt%d" % s3], ["rst"], "hn%d" % s2, "hT")
            S.barrier()
            A.top = p0_top
            XT = [A.alloc([128, S_TOK], BF16) for _ in range(3)]
            Vtok = A.alloc([128, 32, 128], BF16)
            PT = A.alloc([128, 32, 256], BF16)
            accU = A.alloc([128, S_TOK], F32)
            accD = A.alloc([128, S_TOK], F32)
            wq = [A.alloc([128, 8, 384], BF16) for _ in range(2)]
            OTh = A.alloc([128, S_TOK], BF16)
            Tb = [A.alloc([128, 512], F32) for _ in range(3)]
            XTn = ["QT", "KT", "VT"]

            iters = [(h, g) for h in range(8) for g in range(3)]

            def load_w(n):
                h, g = iters[n]
                slot = n % 2
                for part in range(3):
                    c0 = g * 3072 + part * 1024 + h * 128
                    S.dma("pool", lambda e, c0=c0, part=part, slot=slot: e.dma_start(
                        out=wq[slot][:, :, part * 128:(part + 1) * 128],
                        in_=w_qkv[:, c0:c0 + 128].rearrange("(kc p) c -> p kc c", p=128)),
                        w=["wq%d" % slot])

            if DBG['iters'] > 0:
                load_w(0)
            evq = [0]

            def evac(dst, src, rr, ww):
                evq[0] += 1
                if evq[0] % 2:
                    S.op("act", lambda e: e.activation(out=dst, in_=src, func=AF.Copy), r=rr, w=ww)
                else:
                    S.op("dve", lambda e: e.tensor_copy(out=dst, in_=src), r=rr, w=ww)

            inv_sqrt_d = 1.0 / math.sqrt(128.0)
            def do_iter(n, h, g):
                window, dil = PAIRS[g]
                L = S_TOK // dil
                nb = L // 128
                slot = n % 2
                if n + 1 < min(len(iters), DBG['iters']):
                    load_w(n + 1)
                pb = 0
                for part in range(3):
                    for tc in range(8):
                        bank = pb % 2
                        pb += 1

                        def mm(e, part=part, tc=tc, bank=bank, slot=slot):
                            for kc in range(8):
                                ins = e.matmul(ps[bank][:], lhsT=wq[slot][:, kc, part * 128:(part + 1) * 128],
                                               rhs=hT[:, kc, tc * 512:(tc + 1) * 512],
                                               start=(kc == 0), stop=(kc == 7))
                            return ins
                        S.op("pe", mm, r=["wq%d" % slot, "hT"], w=["ps%d" % bank])
                        jn = 512 // dil
                        j0 = tc * jn
                        dst = XT[part].rearrange("p (r j) -> p r j", r=dil)[:, :, j0:j0 + jn]
                        src = ps[bank][:].rearrange("p (j r) -> p r j", r=dil)
                        evac(dst, src, ["ps%d" % bank], [XTn[part]])
                for q4 in range(4):
                    bank = pb % 2
                    pb += 1

                    def trv(e, q4=q4, bank=bank):
                        for i in range(8):
                            b = q4 * 8 + i
                            ins = e.transpose(out=pT(bank)[:, i * 128:(i + 1) * 128],
                                              in_=XT[2][:, b * 128:(b + 1) * 128], identity=idt)
                        return ins
                    S.op("pe", trv, r=["VT", "idt"], w=["ps%d" % bank])
                    evac(Vtok[:, q4 * 8:(q4 + 1) * 8, :], pT(bank).rearrange("p (a b) -> p a b", a=8),
                         ["ps%d" % bank], ["Vtok"])
                slope = 2.0 ** (-8.0 * (g * 8 + h + 1) / 24.0)
                sc = slope * dil * math.sqrt(128.0)
                LAG = 5

                def s_pair(p):
                    b = 2 * p
                    last = ((b + 1) % nb) == nb - 1
                    W = 128 if last else 256
                    WW = 256 + W
                    bank = 2 + p % 4
                    tb = p % 3

                    def mms(e):
                        e.matmul(ps[bank][:, 0:256], lhsT=XT[1][:, b * 128:(b + 1) * 128],
                                 rhs=XT[0][:, b * 128:b * 128 + 256], start=True, stop=True)
                        return e.matmul(ps[bank][:, 256:256 + W], lhsT=XT[1][:, (b + 1) * 128:(b + 2) * 128],
                                        rhs=XT[0][:, (b + 1) * 128:(b + 1) * 128 + W], start=True, stop=True)
                    S.op("pe", mms, r=["QT", "KT"], w=["ps%d" % bank])
                    S.op("dve", lambda e: e.scalar_tensor_tensor(out=Tb[tb][:, :WW], in0=base2[:, :WW], scalar=sc,
                                                                in1=ps[bank][:, :WW],
                                                                op0=ALU.mult, op1=ALU.add),
                         r=["ps%d" % bank, "base"], w=["Tb%d" % tb])
                    S.op("act", lambda e: e.activation(
                        out=PT[:, b:b + 2, :].rearrange("p a b -> p (a b)")[:, :WW], in_=Tb[tb][:, :WW],
                        func=AF.Exp, scale=inv_sqrt_d),
                        r=["Tb%d" % tb], w=["PT%d" % b, "PT%d" % (b + 1)])

                def pv_group(b0, gi):
                    bank = 6 + gi % 2
                    rr = set(["Vtok", "ones"])

                    def mmpv(e):
                        ins = None
                        for which, lhs_of in ((0, lambda bb: Vtok[:, bb, :]), (1, lambda bb: ones)):
                            for i in range(2):
                                b = b0 + i
                                o = ps[bank][:, which * 256 + i * 128: which * 256 + (i + 1) * 128]
                                if b % nb > 0:
                                    e.matmul(o, lhsT=lhs_of(b - 1), rhs=PT[:, b - 1, 128:256], start=True, stop=False)
                                    ins = e.matmul(o, lhsT=lhs_of(b), rhs=PT[:, b, 0:128], start=False, stop=True)
                                else:
                                    ins = e.matmul(o, lhsT=lhs_of(b), rhs=PT[:, b, 0:128], start=True, stop=True)
                        return ins
                    for b in (b0 - 1, b0, b0 + 1):
                        if b >= 0:
                            rr.add("PT%d" % b)
                    S.op("pe", mmpv, r=sorted(rr), w=["psV%d" % (gi % 2)])
                    r_ = b0 // nb
                    n0 = b0 % nb
                    j0 = n0 * 128
                    for which, acc, nm in ((0, accU, "accU"), (1, accD, "accD")):
                        dst = acc.rearrange("p (j r) -> p r j", r=dil)[:, r_, j0:j0 + 256]
                        src = ps[bank][:, which * 256:(which + 1) * 256]
                        if g == 0:
                            S.op("act", lambda e, dst=dst, src=src: e.activation(out=dst, in_=src, func=AF.Copy),
                                 r=["psV%d" % (gi % 2)], w=[nm + "_g0", nm + "_g2"])
                        else:
                            S.op("dve", lambda e, dst=dst, src=src: e.tensor_tensor(out=dst, in0=src, in1=dst, op=ALU.add),
                                 r=["psV%d" % (gi % 2), nm + "_g%d" % (g - 1)], w=[nm + "_g%d" % g])

                LAGP = 2
                for p in range(16 + LAGP):
                    if p < 16:
                        s_pair(p)
                    if p - LAGP >= 0:
                        pv_group(2 * (p - LAGP), p - LAGP)
                if g == 2:
                    S.op("dve", lambda e: e.reciprocal(out=accD, in_=accD), r=["accD_g2"], w=["accD_g2"])
                    S.op("dve", lambda e: e.tensor_tensor(out=OTh, in0=accU, in1=accD, op=ALU.mult),
                         r=["accU_g2", "accD_g2"], w=["OTh"])
                    S.dma("sp", lambda e, h=h: e.dma_start(out=OTs[h * 128:(h + 1) * 128, :], in_=OTh), r=["OTh"])
            for n_, (h_, g_) in enumerate(iters[:DBG['iters']]):
                do_iter(n_, h_, g_)
            S.barrier()
            A.top = const_top
            return

        def ffn_weights(layer, want_wd=True):
            Wgu = A.alloc([128, 8, 2 * DFF], BF16)
            wguv = w_gu[layer].rearrange("(kc p) c -> p kc c", p=128)
            for c in (0, 5, 6, 1, 7, 2, 8, 3, 9, 4, 10):
                S.dma("pool", lambda e, c=c: e.dma_start(out=Wgu[:, :, c * 512:(c + 1) * 512],
                                                        in_=wguv[:, :, c * 512:(c + 1) * 512]), w=["Wgu%d" % c])
            Wd = None
            if want_wd:
                Wd = load_wd(layer)
            return dict(Wgu=Wgu, Wd=Wd, top=A.top)

        def load_wd(layer):
            Wd = A.alloc([128, NF, D], BF16)
            wdv = w_down[layer].rearrange("(f p) c -> p f c", p=128)
            for hf in range(2):
                S.dma("pool", lambda e, hf=hf: e.dma_start(out=Wd[:, hf * 11:(hf + 1) * 11, :],
                                                          in_=wdv[:, hf * 11:(hf + 1) * 11, :]), w=["Wd%d" % hf])
            return Wd

        def phase_outproj(xin, xout, pre_layer=None):
            pre = ffn_weights(pre_layer) if pre_layer is not None else None
            Wo = A.alloc([128, 8, D], BF16)
            OTc = [A.alloc([128, 8, 512], BF16) for _ in range(2)]
            xr = [A.alloc([128, D], F32) for _ in range(4)]
            S.dma("pool", lambda e: e.dma_start(out=Wo, in_=w_attn_out.rearrange("(kc p) c -> p kc c", p=128)),
                  w=["Wo"])
            OTv = OTs.rearrange("(kc p) t -> p kc t", p=128)
            xin_t, xout_t = tiles(xin), tiles(xout)
            pb = 0
            for tc in range(8):
                cs = tc % 2
                S.dma("sp", lambda e, tc=tc, cs=cs: e.dma_start(out=OTc[cs], in_=OTv[:, :, tc * 512:(tc + 1) * 512]),
                      w=["OTc%d" % cs])
                for tt in range(4):
                    T = tc * 4 + tt
                    s2 = T % 4
                    S.dma("sp", lambda e, T=T, s2=s2: e.dma_start(out=xr[s2], in_=xin_t[T]), w=["xr%d" % s2])
                    for half in range(2):
                        bank = pb % 4
                        pb += 1

                        def mm(e, cs=cs, tt=tt, half=half, bank=bank):
                            for kc in range(8):
                                ins = e.matmul(ps[bank][:], lhsT=OTc[cs][:, kc, tt * 128:(tt + 1) * 128],
                                               rhs=Wo[:, kc, half * 512:(half + 1) * 512],
                                               start=(kc == 0), stop=(kc == 7))
                            return ins
                        S.op("pe", mm, r=["OTc%d" % cs, "Wo"], w=["ps%d" % bank])
                        S.op("dve", lambda e, s2=s2, half=half, bank=bank: e.tensor_tensor(
                            out=xr[s2][:, half * 512:(half + 1) * 512], in0=ps[bank][:],
                            in1=xr[s2][:, half * 512:(half + 1) * 512], op=ALU.add),
                            r=["ps%d" % bank, "xr%d" % s2], w=["xr%d" % s2])
                    S.dma("sp", lambda e, T=T, s2=s2: e.dma_start(out=xout_t[T], in_=xr[s2]), r=["xr%d" % s2])
            S.barrier()
            A.top = pre["top"] if pre else const_top
            return pre

        def phase_ffn(layer, xin, xout, pre=None):
            if pre is None:
                pre = ffn_weights(layer)
            Wgu = pre["Wgu"]
            Wd = pre["Wd"] if pre["Wd"] is not None else load_wd(layer)
            gB = A.alloc([128, D], F32)
            ssq = A.alloc([128, 32], F32)
            rst = A.alloc([128, 32], F32)
            junk = A.alloc([128, D], BF16)
            xn = [A.alloc([128, D], F32) for _ in range(4)]
            hn = [A.alloc([128, D], BF16) for _ in range(2)]
            h2T = A.alloc([128, 8, 512], BF16)
            actT = A.alloc([128, NF, 512], BF16)
            sg = [A.alloc([128, 512], F32) for _ in range(2)]
            xr = [A.alloc([128, D], F32) for _ in range(2)]
            S.dma("sp", lambda e: e.dma_start(out=gB, in_=ffn_norm[layer].partition_broadcast(128)), w=["gB"])
            S.op("dve", lambda e: e.memset(ssq, 0.0), w=["ssq%d" % c for c in range(8)])
            xin_t, xout_t = tiles(xin), tiles(xout)

            def norm_chunk(c):
                for tt in range(4):
                    T = c * 4 + tt
                    S.dma("sp", lambda e, T=T, tt=tt: e.dma_start(out=xn[tt], in_=xin_t[T]), w=["xn%d" % tt])
                    sq_accum(xn[tt], junk, ssq[:, T:T + 1], ["xn%d" % tt, "ssq%d" % c], ["ssq%d" % c])
                rstd_from(ssq[:, c * 4:(c + 1) * 4], rst[:, c * 4:(c + 1) * 4], ["ssq%d" % c], ["rst%d" % c])
                for tt in range(4):
                    T = c * 4 + tt
                    normed_transposed(xn[tt], rst[:, T:T + 1], gB, hn[tt % 2], tt % 2,
                                      h2T[:, :, tt * 128:(tt + 1) * 128],
                                      ["xn%d" % tt], ["rst%d" % c], "hn%d" % (tt % 2), "h2T")

            def gate_up(c):
                for f in range(NF):
                    gb, ub, s2 = 2 + f % 2, 4 + f % 2, f % 2
                    cg, cu = f // 4, (NF + f) // 4

                    def mm(e, f=f, gb=gb, ub=ub):
                        for bank, col in ((gb, f * 128), (ub, DFF + f * 128)):
                            for kc in range(8):
                                ins = e.matmul(ps[bank][:], lhsT=Wgu[:, kc, col:col + 128], rhs=h2T[:, kc, :],
                                               start=(kc == 0), stop=(kc == 7))
                        return ins
                    S.op("pe", mm, r=["Wgu%d" % cg, "Wgu%d" % cu, "h2T"], w=["ps%d" % gb, "ps%d" % ub])
                    S.op("act", lambda e, gb=gb, s2=s2: e.activation(out=sg[s2], in_=ps[gb][:], func=AF.Silu),
                         r=["ps%d" % gb], w=["sg%d" % s2])
                    S.op("dve", lambda e, f=f, ub=ub, s2=s2: e.tensor_tensor(out=actT[:, f, :], in0=ps[ub][:],
                                                                            in1=sg[s2], op=ALU.mult),
                         r=["ps%d" % ub, "sg%d" % s2], w=["actT"])

            def down(c):
                for tt in range(4):
                    T = c * 4 + tt
                    s2 = T % 2
                    S.dma("sp", lambda e, T=T, s2=s2: e.dma_start(out=xr[s2], in_=xin_t[T]), w=["xr%d" % s2])
                    for half in range(2):
                        bank = 6 + half

                        def mm(e, tt=tt, half=half, bank=bank):
                            for f in range(NF):
                                ins = e.matmul(ps[bank][:], lhsT=actT[:, f, tt * 128:(tt + 1) * 128],
                                               rhs=Wd[:, f, half * 512:(half + 1) * 512],
                                               start=(f == 0), stop=(f == NF - 1))
                            return ins
                        S.op("pe", mm, r=["actT", "Wd0", "Wd1"], w=["ps%d" % bank])
                        S.op("dve", lambda e, s2=s2, half=half, bank=bank: e.tensor_tensor(
                            out=xr[s2][:, half * 512:(half + 1) * 512], in0=ps[bank][:],
                            in1=xr[s2][:, half * 512:(half + 1) * 512], op=ALU.add),
                            r=["ps%d" % bank, "xr%d" % s2], w=["xr%d" % s2])
                    S.dma("sp", lambda e, T=T, s2=s2: e.dma_start(out=xout_t[T], in_=xr[s2]), r=["xr%d" % s2])

            norm_chunk(0)
            for c in range(8):
                gate_up(c)
                if c + 1 < 8:
                    norm_chunk(c + 1)
                down(c)
            S.barrier()
            A.top = const_top

        def phase_pool(xin, xout, pre_layer=None):
            pre = ffn_weights(pre_layer, want_wd=False) if pre_layer is not None else None
            Win = A.alloc([128, 8, D], BF16)
            Wg = A.alloc([128, 4, 2, 256], BF16)
            gB = A.alloc([128, D], F32)
            scB = A.alloc([128, D], F32)
            rc = A.alloc([128, 4, 16], F32)
            ssq = A.alloc([128, 32], F32)
            rst = A.alloc([128, 32], F32)
            junk = A.alloc([128, D], BF16)
            xn = [A.alloc([128, D], F32) for _ in range(4)]
            hn = [A.alloc([128, D], BF16) for _ in range(2)]
            hTc = [A.alloc([128, 8, 512], BF16) for _ in range(2)]
            uT = A.alloc([128, 8, 528], F32)
            chA = [A.alloc([128, 528], F32) for _ in range(2)]
            chB = [A.alloc([128, 528], F32) for _ in range(2)]
            yT = A.alloc([128, 8, 512], BF16)
            tmp = [A.alloc([128, 512], F32) for _ in range(2)]
            tmp16 = A.alloc([128, 16], F32)
            xr = [A.alloc([128, D], F32) for _ in range(2)]
            S.dma("pool", lambda e: e.dma_start(out=Win, in_=w_pool_in.rearrange("(kc p) c -> p kc c", p=128)),
                  w=["Win"])
            for gi in range(4):
                S.dma("pool", lambda e, gi=gi: e.dma_start(
                    out=Wg[:, gi, :, :], in_=w_pool_group[gi].rearrange("(kc p) d -> p kc d", p=128)), w=["Wg"])
            S.dma("sp", lambda e: e.dma_start(out=gB, in_=pool_norm.partition_broadcast(128)), w=["gB"])
            S.dma("sp", lambda e: e.dma_start(out=scB, in_=pool_scale.partition_broadcast(128)), w=["scB"])
            S.dma("sp", lambda e: e.dma_start(out=rc.rearrange("p a b -> p (a b)"),
                                              in_=c_rc.rearrange("a b -> (a b)").partition_broadcast(128)), w=["rc"])
            S.op("dve", lambda e: e.memset(ssq, 0.0), w=["ssq%d" % t for t in range(8)])
            S.op("dve", lambda e: e.memset(uT, 0.0), w=["uT%d" % ct for ct in range(8)])
            xin_t, xout_t = tiles(xin), tiles(xout)
            xq = [0]

            def norm_chunk(c):
                cs = c % 2
                for tt in range(4):
                    T = c * 4 + tt
                    S.dma("sp", lambda e, T=T, tt=tt: e.dma_start(out=xn[tt], in_=xin_t[T]), w=["xn%d" % tt])
                    sq_accum(xn[tt], junk, ssq[:, T:T + 1], ["xn%d" % tt, "ssq%d" % c], ["ssq%d" % c])
                rstd_from(ssq[:, c * 4:(c + 1) * 4], rst[:, c * 4:(c + 1) * 4], ["ssq%d" % c], ["rst%d" % c])
                for tt in range(4):
                    T = c * 4 + tt
                    normed_transposed(xn[tt], rst[:, T:T + 1], gB, hn[T % 2], T % 2,
                                      hTc[cs][:, :, tt * 128:(tt + 1) * 128],
                                      ["xn%d" % tt], ["rst%d" % c], "hn%d" % (T % 2), "hTc%d" % cs)

            def mix_chunk(c):
                cs = c % 2
                if c > 0:
                    S.op("dve", lambda e: e.tensor_copy(out=uT[:, :, 0:16], in_=uT[:, :, 512:528]),
                         r=["uT%d" % ct for ct in range(8)], w=["uT%d" % ct for ct in range(8)])
                for ct in range(8):
                    bank = 2 + ct % 2

                    def mm(e, ct=ct, bank=bank, cs=cs):
                        for kc in range(8):
                            ins = e.matmul(ps[bank][:], lhsT=Win[:, kc, ct * 128:(ct + 1) * 128], rhs=hTc[cs][:, kc, :],
                                           start=(kc == 0), stop=(kc == 7))
                        return ins
                    S.op("pe", mm, r=["Win", "hTc%d" % cs], w=["ps%d" % bank])
                    S.op("act", lambda e, ct=ct, bank=bank: e.activation(out=uT[:, ct, 16:528], in_=ps[bank][:],
                                                                       func=AF.Copy),
                         r=["ps%d" % bank], w=["uT%d" % ct])
                    gi = ct // 2
                    wdw = 2 ** (gi + 1)
                    cur = uT[:, ct, :]
                    cur_res = "uT%d" % ct
                    bufs = [chA[ct % 2], chB[ct % 2]]
                    bres = ["chA%d" % (ct % 2), "chB%d" % (ct % 2)]
                    for lev in range(gi + 1):
                        sh = 2 ** lev
                        lo = 2 * sh - 1
                        dstb = bufs[lev % 2]
                        S.op("dve" if gi >= 2 else "pool", lambda e, dstb=dstb, cur=cur, lo=lo, sh=sh: e.tensor_tensor(
                            out=dstb[:, lo:528], in0=cur[:, lo:528], in1=cur[:, lo - sh:528 - sh], op=ALU.add),
                            r=[cur_res], w=[bres[lev % 2]])
                        cur = dstb
                        cur_res = bres[lev % 2]
                    S.op("dve", lambda e, ct=ct, cur=cur, wdw=wdw: e.scalar_tensor_tensor(
                        out=yT[:, ct, :], in0=cur[:, 16:528], scalar=1.0 / wdw, in1=uT[:, ct, 16:528],
                        op0=ALU.mult, op1=ALU.subtract), r=[cur_res, "uT%d" % ct], w=["yT"])
                    if c == 0:
                        S.op("dve", lambda e, cur=cur, gi=gi: e.tensor_tensor(out=tmp16, in0=cur[:, 16:32],
                                                                            in1=rc[:, gi, :], op=ALU.mult),
                             r=[cur_res, "rc"], w=["tmp16"])
                        S.op("dve", lambda e, ct=ct: e.tensor_tensor(out=yT[:, ct, 0:16], in0=tmp16,
                                                                    in1=uT[:, ct, 16:32], op=ALU.subtract),
                             r=["tmp16", "uT%d" % ct, "yT"], w=["yT"])
                for tt in range(4):
                    T = c * 4 + tt
                    s2 = T % 2
                    S.dma("sp", lambda e, T=T, s2=s2: e.dma_start(out=xr[s2], in_=xin_t[T]), w=["xr%d" % s2])
                    for half in range(2):
                        bank = 4 + (T * 2 + half) % 4

                        def mm(e, tt=tt, half=half, bank=bank):
                            for gg in range(2):
                                gi = half * 2 + gg
                                for k2 in range(2):
                                    ins = e.matmul(ps[bank][:, gg * 256:(gg + 1) * 256],
                                                   lhsT=yT[:, gi * 2 + k2, tt * 128:(tt + 1) * 128],
                                                   rhs=Wg[:, gi, k2, :], start=(k2 == 0), stop=(k2 == 1))
                            return ins
                        S.op("pe", mm, r=["yT", "Wg"], w=["ps%d" % bank])
                        S.op("dve", lambda e, half=half, bank=bank: e.tensor_tensor(
                            out=tmp[half], in0=ps[bank][:], in1=scB[:, half * 512:(half + 1) * 512], op=ALU.mult),
                            r=["ps%d" % bank, "scB"], w=["tmp%d" % half])
                        S.op("pool", lambda e, s2=s2, half=half: e.tensor_tensor(
                            out=xr[s2][:, half * 512:(half + 1) * 512], in0=tmp[half],
                            in1=xr[s2][:, half * 512:(half + 1) * 512], op=ALU.add),
                            r=["tmp%d" % half, "xr%d" % s2], w=["xr%d" % s2])
                    S.dma("sp", lambda e, T=T, s2=s2: e.dma_start(out=xout_t[T], in_=xr[s2]), r=["xr%d" % s2])

            norm_chunk(0)
            for c in range(8):
                if c + 1 < 8:
                    norm_chunk(c + 1)
                mix_chunk(c)
            S.barrier()
            A.top = pre["top"] if pre else const_top
            return pre

        def phase_final(xin, xout):
            gB = A.alloc([128, D], F32)
            ssq = A.alloc([128, 32], F32)
            rst = A.alloc([128, 32], F32)
            junk = A.alloc([128, D], BF16)
            xn = [A.alloc([128, D], F32) for _ in range(4)]
            yo = [A.alloc([128, D], F32) for _ in range(3)]
            S.dma("sp", lambda e: e.dma_start(out=gB, in_=final_norm.partition_broadcast(128)), w=["gB"])
            S.op("dve", lambda e: e.memset(ssq, 0.0), w=["ssq"])
            xin_t, xout_t = tiles(xin), tiles(xout)
            for T in range(32):
                s4 = T % 4
                S.dma("sp", lambda e, T=T, s4=s4: e.dma_start(out=xn[s4], in_=xin_t[T]), w=["xn%d" % s4])
                sq_accum(xn[s4], junk, ssq[:, T:T + 1], ["xn%d" % s4, "ssq"], ["ssq"])
            rstd_from(ssq, rst, ["ssq"], ["rst"])
            for T in range(32):
                s4, s3 = T % 4, T % 3
                S.dma("sp", lambda e, T=T, s4=s4: e.dma_start(out=xn[s4], in_=xin_t[T]), w=["xn%d" % s4])
                S.op("dve", lambda e, T=T, s4=s4, s3=s3: e.scalar_tensor_tensor(
                    out=yo[s3], in0=xn[s4], scalar=rst[:, T:T + 1], in1=gB, op0=ALU.mult, op1=ALU.mult),
                    r=["xn%d" % s4, "rst", "gB"], w=["yo%d" % s3])
                S.dma("sp", lambda e, T=T, s3=s3: e.dma_start(out=xout_t[T], in_=yo[s3]), r=["yo%d" % s3])
            S.barrier()
            A.top = const_top

        def copy_out(src):
            for i in range(8):
                S.dma("sp", lambda e, i=i: e.dma_start(
                    out=out[i * 512:(i + 1) * 512, :].rearrange("(p a) d -> p (a d)", p=128),
                    in_=src[i * 512:(i + 1) * 512, :].rearrange("(p a) d -> p (a d)", p=128)))

        carry = {}
        stages = [
            ("attn", lambda: (phase_attention(x, xs[0]), carry.__setitem__("f0", phase_outproj(x, xs[0], pre_layer=0))), xs[0]),
            ("ffn0", lambda: phase_ffn(0, xs[0], xs[1], pre=carry.get("f0")), xs[1]),
            ("pool", lambda: carry.__setitem__("f1", phase_pool(xs[1], xs[2], pre_layer=1)), xs[2]),
            ("ffn1", lambda: phase_ffn(1, xs[2], xs[3], pre=carry.get("f1")), xs[3]),
            ("final", lambda: phase_final(xs[3], out), None),
        ]
        if DBG.get('nostage'):
            copy_out(x)
            stages = []
        for name, fn, res in stages:
            fn()
            if stop_after == name and res is not None:
                copy_out(res)
                break
        S.finish()
        S.emit()
    return nc


def _consts():
    ident = np.eye(128, dtype=np.float32).astype(ml_dtypes.bfloat16)
    k = np.arange(128)[:, None]
    q = np.arange(128)[None, :]
    cur = np.where(q - k >= 0, -(q - k).astype(np.float32), NEG)
    prev = np.where(k >= q, -(128 + q - k).astype(np.float32), NEG)
    base = np.concatenate([cur, prev], axis=1).astype(np.float32)
    rc = np.zeros((4, 16), np.float32)
    for gi, w in enumerate((2, 4, 8, 16)):
        rc[gi] = 1.0 / np.minimum(np.arange(1, 17), w)
    return ident, base, rc


_NC_CACHE = {}


def kernel(x, attn_norm, w_qkv, w_attn_out, pool_norm, w_pool_in, w_pool_group, pool_scale,
           ffn_norm, w_ffn_gate_up, w_ffn_down, final_norm, _stop_after=None):
    f = lambda a: np.ascontiguousarray(np.asarray(a), dtype=np.float32)
    x = f(x)
    ident, base, rc = _consts()
    shared = dict(
        attn_norm=f(attn_norm)[0], w_qkv=f(w_qkv)[0], w_attn_out=f(w_attn_out)[0],
        pool_norm=f(pool_norm)[0], w_pool_in=f(w_pool_in)[0], w_pool_group=f(w_pool_group)[0],
        pool_scale=f(pool_scale)[0], ffn_norm=f(ffn_norm), w_ffn_gate_up=f(w_ffn_gate_up),
        w_ffn_down=f(w_ffn_down), final_norm=f(final_norm),
        c_ident=ident, c_base=base, c_rc=rc,
    )
    if _stop_after not in _NC_CACHE:
        _NC_CACHE[_stop_after] = build_nc(_stop_after)
    nc = _NC_CACHE[_stop_after]
    ncores = DBG.get('ncores', 8)
    in_maps = [dict(shared, x=x[b]) for b in range(ncores)]
    res = run_bass_kernel_spmd(nc, in_maps, core_ids=list(range(ncores)))
    return np.stack([np.asarray(r["out"], dtype=np.float32) for r in res.results], axis=0)
```

```python
import contextlib
import math
import numpy as np
import ml_dtypes
import concourse.bass as bass
import concourse.mybir as mybir
from concourse.bass_utils import run_bass_kernel_spmd

F32 = mybir.dt.float32
BF16 = mybir.dt.bfloat16
AF = mybir.ActivationFunctionType
ALU = mybir.AluOpType

ENGS = ("pe", "act", "dve", "pool", "sp")
DMA_RING = 12

S_TOK = 4096
D = 1024
DFF = 2816
NF = DFF // 128
QKVW = 9216
PAIRS = ((128, 1), (512, 4), (2048, 16))
EPS = 1e-6
NEG = -1.0e5
DBG = dict(iters=24, attn_blocks=True, proj=True)
STRICT = False


class Sched:
    def __init__(self, nc):
        self.nc = nc
        self.ops = {e: [] for e in ENGS}
        self.last_w = {}
        self.readers = {}
        self.ndma = {e: 0 for e in ENGS}

    def _deps(self, eng, r, w):
        deps = set()
        for res in r:
            t = self.last_w.get(res)
            if t is not None:
                deps.add((t, "raw"))
        for res in w:
            t = self.last_w.get(res)
            if t is not None:
                deps.add((t, "waw"))
            for t in self.readers.get(res, ()):
                deps.add((t, "war"))
        out = set()
        for t, kind in deps:
            if t[0] == "eng" and t[1] == eng:
                if eng in ("pe", "sp") or (kind != "raw" and not STRICT):
                    continue
            out.add(t)
        return out

    def _commit(self, tok, r, w):
        for res in r:
            self.readers.setdefault(res, []).append(tok)
        for res in w:
            self.last_w[res] = tok
            self.readers[res] = []

    def op(self, eng, fn, r=(), w=()):
        idx = len(self.ops[eng])
        deps = self._deps(eng, r, w)
        tok = ("eng", eng, idx)
        self.ops[eng].append(dict(fn=fn, deps=deps, dma=None, sig=False))
        self._commit(tok, r, w)
        return tok

    def dma(self, eng, fn, r=(), w=()):
        deps = self._deps(eng, r, w)
        n = self.ndma[eng]
        self.ndma[eng] += 1
        tok = ("dma", eng, n)
        if n >= DMA_RING:
            deps.add(("dma", eng, n - DMA_RING))
        self.ops[eng].append(dict(fn=fn, deps=deps, dma=n, sig=False))
        self._commit(tok, r, w)
        return tok

    def _tail_tokens(self):
        toks = []
        for e in ENGS:
            for i in range(len(self.ops[e]) - 1, -1, -1):
                o = self.ops[e][i]
                if o["dma"] is None and o["fn"] is not None:
                    toks.append(("eng", e, i))
                    break
            n = self.ndma[e]
            for k in range(max(0, n - DMA_RING), n):
                toks.append(("dma", e, k))
        return toks

    def barrier(self):
        toks = self._tail_tokens()
        self.last_w = {}
        self.readers = {}
        for e in ENGS:
            deps = set(t for t in toks if not (t[0] == "eng" and t[1] == e and e in ("pe", "sp")))
            self.ops[e].append(dict(fn=None, deps=deps, dma=None, sig=False))

    def finish(self, eng="sp"):
        deps = set(t for t in self._tail_tokens() if t[0] == "dma")
        self.ops[eng].append(dict(fn=None, deps=deps, dma=None, sig=False))

    def emit(self):
        nc = self.nc
        for e in ENGS:
            for o in self.ops[e]:
                for t in o["deps"]:
                    if t[0] == "eng":
                        self.ops[t[1]][t[2]]["sig"] = True
        cnt = {}
        for e in ENGS:
            c = 0
            for i, o in enumerate(self.ops[e]):
                if o["sig"]:
                    assert o["fn"] is not None
                    c += 1
                cnt[(e, i)] = c
        with contextlib.ExitStack() as st:
            esem = {e: st.enter_context(nc.semaphore("prog_" + e)) for e in ENGS}
            dsem = {}
            for e in ENGS:
                if self.ndma[e]:
                    dsem[e] = [st.enter_context(nc.semaphore("dq_%s_%d" % (e, k)))
                               for k in range(min(DMA_RING, self.ndma[e]))]
            block = st.enter_context(nc.Block())
            ops = self.ops

            def run(e, eng):
                seen = {}
                for o in ops[e]:
                    need = {}
                    for t in o["deps"]:
                        if t[0] == "eng":
                            key = ("eng", t[1])
                            val = cnt[(t[1], t[2])]
                            sem = esem[t[1]]
                        else:
                            key = ("dma", t[1], t[2] % DMA_RING)
                            val = 16 * (t[2] // DMA_RING + 1)
                            sem = dsem[t[1]][t[2] % DMA_RING]
                        if seen.get(key, 0) >= val:
                            continue
                        if key not in need or need[key][1] < val:
                            need[key] = (sem, val)
                    for key, (sem, val) in need.items():
                        eng.wait_ge(sem, val)
                        seen[key] = val
                    if o["fn"] is None:
                        continue
                    ins = o["fn"](eng)
                    if o["dma"] is not None:
                        ins.then_inc(dsem[e][o["dma"] % DMA_RING], 16)
                    elif o["sig"]:
                        ins.then_inc(esem[e], 1)

            @block.tensor
            def _(eng):
                run("pe", eng)

            @block.scalar
            def _(eng):
                run("act", eng)

            @block.vector
            def _(eng):
                run("dve", eng)

            @block.gpsimd
            def _(eng):
                run("pool", eng)

            @block.sync
            def _(eng):
                run("sp", eng)


class Arena:
    def __init__(self, big):
        self.big = big
        self.cap = big.shape[1]
        self.top = 0

    def alloc(self, shape, dt):
        n = 1
        for s in shape[1:]:
            n *= s
        e16 = n * (2 if dt == F32 else 1)
        off = self.top
        self.top += (e16 + 15) // 16 * 16
        assert self.top <= self.cap, "SBUF arena overflow: %d > %d" % (self.top, self.cap)
        ap = self.big[0:shape[0], off:off + e16]
        if dt == F32:
            ap = ap.bitcast(F32)
        if len(shape) == 3:
            ap = ap.rearrange("p (a b) -> p a b", a=shape[1])
        elif len(shape) == 4:
            ap = ap.rearrange("p (a b c) -> p a b c", a=shape[1], b=shape[2])
        return ap


def build_nc(stop_after=None):
    nc = bass.Bass("TRN2", target_bir_lowering=False)

    def din(name, shape, dt=F32):
        return nc.dram_tensor(name, list(shape), dt, kind="ExternalInput").ap()

    x = din("x", [S_TOK, D])
    attn_norm = din("attn_norm", [D])
    w_qkv = din("w_qkv", [D, QKVW])
    w_attn_out = din("w_attn_out", [D, D])
    pool_norm = din("pool_norm", [D])
    w_pool_in = din("w_pool_in", [D, D])
    w_pool_group = din("w_pool_group", [4, 256, 256])
    pool_scale = din("pool_scale", [D])
    ffn_norm = din("ffn_norm", [2, D])
    w_gu = din("w_ffn_gate_up", [2, D, 2 * DFF])
    w_down = din("w_ffn_down", [2, DFF, D])
    final_norm = din("final_norm", [D])
    c_ident = din("c_ident", [128, 128], BF16)
    c_base = din("c_base", [128, 256])
    c_rc = din("c_rc", [4, 16])
    out = nc.dram_tensor("out", [S_TOK, D], F32, kind="ExternalOutput").ap()

    xs = [nc.dram_tensor("xs%d" % i, [S_TOK, D], F32).ap() for i in range(4)]
    OTs = nc.dram_tensor("ots", [D, S_TOK], BF16).ap()

    st = contextlib.ExitStack()
    with st:
        big = st.enter_context(nc.sbuf_tensor("big", [128, 105984], BF16))
        ps = [st.enter_context(nc.psum_tensor("ps%d" % i, [128, 512], F32)) for i in range(8)]
        S = Sched(nc)
        A = Arena(big)

        def tiles(ap):
            return ap.rearrange("(n p) d -> n p d", p=128)

        idt = A.alloc([128, 128], BF16)
        ones = A.alloc([128, 128], BF16)
        base2 = A.alloc([128, 512], F32)
        S.dma("sp", lambda e: e.dma_start(out=idt, in_=c_ident), w=["idt"])
        S.dma("sp", lambda e: e.dma_start(out=base2[:, 0:256], in_=c_base), w=["base"])
        S.dma("sp", lambda e: e.dma_start(out=base2[:, 256:512], in_=c_base), w=["base"])
        S.op("dve", lambda e: e.memset(ones, 1.0), w=["ones"])
        const_top = A.top

        def pT(b):
            return ps[b][:].bitcast(BF16)

        def sq_accum(xt_ap, junk, ssq_col, rx, wss):
            S.op("act", lambda e: e.activation(out=junk, in_=xt_ap, func=AF.Square, accum_out=ssq_col),
                 r=rx, w=["junk"] + wss)

        def rstd_from(ssq_ap, rst_ap, rr, ww):
            S.op("dve", lambda e: e.tensor_scalar(out=rst_ap, in0=ssq_ap, scalar1=1.0 / D, scalar2=EPS,
                                                 op0=ALU.mult, op1=ALU.add), r=rr, w=ww)
            S.op("act", lambda e: e.activation(out=rst_ap, in_=rst_ap, func=AF.Sqrt), r=ww, w=ww)
            S.op("dve", lambda e: e.reciprocal(out=rst_ap, in_=rst_ap), r=ww, w=ww)

        def normed_transposed(xt_ap, rst_col, gB, hn_ap, bank, dst_ap, rx, rrst, hn_res, dst_res):
            S.op("dve", lambda e: e.scalar_tensor_tensor(out=hn_ap, in0=xt_ap, scalar=rst_col, in1=gB,
                                                        op0=ALU.mult, op1=ALU.mult),
                 r=rx + rrst + ["gB"], w=[hn_res])

            def tr(e):
                for kc in range(8):
                    ins = e.transpose(out=pT(bank)[:, kc * 128:(kc + 1) * 128],
                                      in_=hn_ap[:, kc * 128:(kc + 1) * 128], identity=idt)
                return ins
            S.op("pe", tr, r=[hn_res, "idt"], w=["ps%d" % bank])
            S.op("act", lambda e: e.activation(out=dst_ap,
                                               in_=pT(bank).rearrange("p (a b) -> p a b", a=8),
                                               func=AF.Copy),
                 r=["ps%d" % bank], w=[dst_res])

        def phase_attention(xin, xout):
            hT = A.alloc([128, 8, S_TOK], BF16)
            gB = A.alloc([128, D], F32)
            ssq = A.alloc([128, 32], F32)
            rst = A.alloc([128, 32], F32)
            junk = A.alloc([128, D], BF16)
            p0_top = A.top
            xt = [A.alloc([128, D], F32) for _ in range(3)]
            hn = [A.alloc([128, D], BF16) for _ in range(2)]
            S.dma("sp", lambda e: e.dma_start(out=gB, in_=attn_norm.partition_broadcast(128)), w=["gB"])
            S.op("dve", lambda e: e.memset(ssq, 0.0), w=["ssq"])
            xin_t = tiles(xin)
            for tt in range(32):
                s3 = tt % 3
                S.dma("sp", lambda e, tt=tt, s3=s3: e.dma_start(out=xt[s3], in_=xin_t[tt]), w=["xt%d" % s3])
                sq_accum(xt[s3], junk, ssq[:, tt:tt + 1], ["xt%d" % s3, "ssq"], ["ssq"])
            rstd_from(ssq, rst, ["ssq"], ["rst"])
            for tt in range(32):
                s3, s2 = (tt + 32) % 3, tt % 2
                S.dma("sp", lambda e, tt=tt, s3=s3: e.dma_start(out=xt[s3], in_=xin_t[tt]), w=["xt%d" % s3])
                normed_transposed(xt[s3], rst[:, tt:tt + 1], gB, hn[s2], tt % 2,
                                  hT[:, :, tt * 128:(tt + 1) * 128],
                                  ["xt%d" % s3], ["rst"], "hn%d" % s2, "hT")
            S.barrier()
            A.top = p0_top
            XT = [A.alloc([128, S_TOK], BF16) for _ in range(3)]
            Vtok = A.alloc([128, 32, 128], BF16)
            PT = A.alloc([128, 32, 256], BF16)
            accU = A.alloc([128, S_TOK], F32)
            accD = A.alloc([128, S_TOK], F32)
            wq = [A.alloc([128, 8, 384], BF16) for _ in range(2)]
            OTh = A.alloc([128, S_TOK], BF16)
            Tb = [A.alloc([128, 512], F32) for _ in range(3)]
            XTn = ["QT", "KT", "VT"]

            iters = [(h, g) for h in range(8) for g in range(3)]

            def load_w(n):
                h, g = iters[n]
                slot = n % 2
                for part in range(3):
                    c0 = g * 3072 + part * 1024 + h * 128
                    S.dma("pool", lambda e, c0=c0, part=part, slot=slot: e.dma_start(
                        out=wq[slot][:, :, part * 128:(part + 1) * 128],
                        in_=w_qkv[:, c0:c0 + 128].rearrange("(kc p) c -> p kc c", p=128)),
                        w=["wq%d" % slot])

            if DBG['iters'] > 0:
                load_w(0)
            evq = [0]

            def evac(dst, src, rr, ww):
                evq[0] += 1
                if evq[0] % 2:
                    S.op("act", lambda e: e.activation(out=dst, in_=src, func=AF.Copy), r=rr, w=ww)
                else:
                    S.op("dve", lambda e: e.tensor_copy(out=dst, in_=src), r=rr, w=ww)

            inv_sqrt_d = 1.0 / math.sqrt(128.0)
            def do_iter(n, h, g):
                window, dil = PAIRS[g]
                L = S_TOK // dil
                nb = L // 128
                slot = n % 2
                if n + 1 < min(len(iters), DBG['iters']):
                    load_w(n + 1)
                pb = 0
                for part in range(3):
                    for tc in range(8):
                        bank = pb % 2
                        pb += 1

                        def mm(e, part=part, tc=tc, bank=bank, slot=slot):
                            for kc in range(8):
                                ins = e.matmul(ps[bank][:], lhsT=wq[slot][:, kc, part * 128:(part + 1) * 128],
                                               rhs=hT[:, kc, tc * 512:(tc + 1) * 512],
                                               start=(kc == 0), stop=(kc == 7))
                            return ins
                        S.op("pe", mm, r=["wq%d" % slot, "hT"], w=["ps%d" % bank])
                        jn = 512 // dil
                        j0 = tc * jn
                        dst = XT[part].rearrange("p (r j) -> p r j", r=dil)[:, :, j0:j0 + jn]
                        src = ps[bank][:].rearrange("p (j r) -> p r j", r=dil)
                        evac(dst, src, ["ps%d" % bank], [XTn[part]])
                for q4 in range(4):
                    bank = pb % 2
                    pb += 1

                    def trv(e, q4=q4, bank=bank):
                        for i in range(8):
                            b = q4 * 8 + i
                            ins = e.transpose(out=pT(bank)[:, i * 128:(i + 1) * 128],
                                              in_=XT[2][:, b * 128:(b + 1) * 128], identity=idt)
                        return ins
                    S.op("pe", trv, r=["VT", "idt"], w=["ps%d" % bank])
                    evac(Vtok[:, q4 * 8:(q4 + 1) * 8, :], pT(bank).rearrange("p (a b) -> p a b", a=8),
                         ["ps%d" % bank], ["Vtok"])
                slope = 2.0 ** (-8.0 * (g * 8 + h + 1) / 24.0)
                sc = slope * dil * math.sqrt(128.0)
                LAG = 5

                def s_pair(p):
                    b = 2 * p
                    last = ((b + 1) % nb) == nb - 1
                    W = 128 if last else 256
                    WW = 256 + W
                    bank = 2 + p % 4
                    tb = p % 3

                    def mms(e):
                        e.matmul(ps[bank][:, 0:256], lhsT=XT[1][:, b * 128:(b + 1) * 128],
                                 rhs=XT[0][:, b * 128:b * 128 + 256], start=True, stop=True)
                        return e.matmul(ps[bank][:, 256:256 + W], lhsT=XT[1][:, (b + 1) * 128:(b + 2) * 128],
                                        rhs=XT[0][:, (b + 1) * 128:(b + 1) * 128 + W], start=True, stop=True)
                    S.op("pe", mms, r=["QT", "KT"], w=["ps%d" % bank])
                    S.op("dve", lambda e: e.scalar_tensor_tensor(out=Tb[tb][:, :WW], in0=base2[:, :WW], scalar=sc,
                                                                in1=ps[bank][:, :WW],
                                                                op0=ALU.mult, op1=ALU.add),
                         r=["ps%d" % bank, "base"], w=["Tb%d" % tb])
                    S.op("act", lambda e: e.activation(
                        out=PT[:, b:b + 2, :].rearrange("p a b -> p (a b)")[:, :WW], in_=Tb[tb][:, :WW],
                        func=AF.Exp, scale=inv_sqrt_d),
                        r=["Tb%d" % tb], w=["PT%d" % b, "PT%d" % (b + 1)])

                def pv_group(b0, gi):
                    bank = 6 + gi % 2
                    rr = set(["Vtok", "ones"])

                    def mmpv(e):
                        ins = None
                        for which, lhs_of in ((0, lambda bb: Vtok[:, bb, :]), (1, lambda bb: ones)):
                            for i in range(2):
                                b = b0 + i
                                o = ps[bank][:, which * 256 + i * 128: which * 256 + (i + 1) * 128]
                                if b % nb > 0:
                                    e.matmul(o, lhsT=lhs_of(b - 1), rhs=PT[:, b - 1, 128:256], start=True, stop=False)
                                    ins = e.matmul(o, lhsT=lhs_of(b), rhs=PT[:, b, 0:128], start=False, stop=True)
                                else:
                                    ins = e.matmul(o, lhsT=lhs_of(b), rhs=PT[:, b, 0:128], start=True, stop=True)
                        return ins
                    for b in (b0 - 1, b0, b0 + 1):
                        if b >= 0:
                            rr.add("PT%d" % b)
                    S.op("pe", mmpv, r=sorted(rr), w=["psV%d" % (gi % 2)])
                    r_ = b0 // nb
                    n0 = b0 % nb
                    j0 = n0 * 128
                    for which, acc, nm in ((0, accU, "accU"), (1, accD, "accD")):
                        dst = acc.rearrange("p (j r) -> p r j", r=dil)[:, r_, j0:j0 + 256]
                        src = ps[bank][:, which * 256:(which + 1) * 256]
                        if g == 0:
                            S.op("act", lambda e, dst=dst, src=src: e.activation(out=dst, in_=src, func=AF.Copy),
                                 r=["psV%d" % (gi % 2)], w=[nm + "_g0", nm + "_g2"])
                        else:
                            S.op("dve", lambda e, dst=dst, src=src: e.tensor_tensor(out=dst, in0=src, in1=dst, op=ALU.add),
                                 r=["psV%d" % (gi % 2), nm + "_g%d" % (g - 1)], w=[nm + "_g%d" % g])

                LAGP = 2
                for p in range(16 + LAGP):
                    if p < 16:
                        s_pair(p)
                    if p - LAGP >= 0:
                        pv_group(2 * (p - LAGP), p - LAGP)
                if g == 2:
                    S.op("dve", lambda e: e.reciprocal(out=accD, in_=accD), r=["accD_g2"], w=["accD_g2"])
                    S.op("dve", lambda e: e.tensor_tensor(out=OTh, in0=accU, in1=accD, op=ALU.mult),
                         r=["accU_g2", "accD_g2"], w=["OTh"])
                    S.dma("sp", lambda e, h=h: e.dma_start(out=OTs[h * 128:(h + 1) * 128, :], in_=OTh), r=["OTh"])
            for n_, (h_, g_) in enumerate(iters[:DBG['iters']]):
                do_iter(n_, h_, g_)
            S.barrier()
            A.top = const_top
            return

        def ffn_weights(layer, want_wd=True):
            Wgu = A.alloc([128, 8, 2 * DFF], BF16)
            wguv = w_gu[layer].rearrange("(kc p) c -> p kc c", p=128)
            for c in (0, 5, 6, 1, 7, 2, 8, 3, 9, 4, 10):
                S.dma("pool", lambda e, c=c: e.dma_start(out=Wgu[:, :, c * 512:(c + 1) * 512],
                                                        in_=wguv[:, :, c * 512:(c + 1) * 512]), w=["Wgu%d" % c])
            Wd = None
            if want_wd:
                Wd = load_wd(layer)
            return dict(Wgu=Wgu, Wd=Wd, top=A.top)

        def load_wd(layer):
            Wd = A.alloc([128, NF, D], BF16)
            wdv = w_down[layer].rearrange("(f p) c -> p f c", p=128)
            for hf in range(2):
                S.dma("pool", lambda e, hf=hf: e.dma_start(out=Wd[:, hf * 11:(hf + 1) * 11, :],
                                                          in_=wdv[:, hf * 11:(hf + 1) * 11, :]), w=["Wd%d" % hf])
            return Wd

        def phase_outproj(xin, xout, pre_layer=None):
            pre = ffn_weights(pre_layer) if pre_layer is not None else None
            Wo = A.alloc([128, 8, D], BF16)
            OTc = [A.alloc([128, 8, 512], BF16) for _ in range(2)]
            xr = [A.alloc([128, D], F32) for _ in range(4)]
            S.dma("pool", lambda e: e.dma_start(out=Wo, in_=w_attn_out.rearrange("(kc p) c -> p kc c", p=128)),
                  w=["Wo"])
            OTv = OTs.rearrange("(kc p) t -> p kc t", p=128)
            xin_t, xout_t = tiles(xin), tiles(xout)
            pb = 0
            for tc in range(8):
                cs = tc % 2
                S.dma("sp", lambda e, tc=tc, cs=cs: e.dma_start(out=OTc[cs], in_=OTv[:, :, tc * 512:(tc + 1) * 512]),
                      w=["OTc%d" % cs])
                for tt in range(4):
                    T = tc * 4 + tt
                    s2 = T % 4
                    S.dma("sp", lambda e, T=T, s2=s2: e.dma_start(out=xr[s2], in_=xin_t[T]), w=["xr%d" % s2])
                    for half in range(2):
                        bank = pb % 4
                        pb += 1

                        def mm(e, cs=cs, tt=tt, half=half, bank=bank):
                            for kc in range(8):
                                ins = e.matmul(ps[bank][:], lhsT=OTc[cs][:, kc, tt * 128:(tt + 1) * 128],
                                               rhs=Wo[:, kc, half * 512:(half + 1) * 512],
                                               start=(kc == 0), stop=(kc == 7))
                            return ins
                        S.op("pe", mm, r=["OTc%d" % cs, "Wo"], w=["ps%d" % bank])
                        S.op("dve", lambda e, s2=s2, half=half, bank=bank: e.tensor_tensor(
                            out=xr[s2][:, half * 512:(half + 1) * 512], in0=ps[bank][:],
                            in1=xr[s2][:, half * 512:(half + 1) * 512], op=ALU.add),
                            r=["ps%d" % bank, "xr%d" % s2], w=["xr%d" % s2])
                    S.dma("pool", lambda e, T=T, s2=s2: e.dma_start(out=xout_t[T], in_=xr[s2]), r=["xr%d" % s2])
            S.barrier()
            A.top = pre["top"] if pre else const_top
            return pre

        def phase_ffn(layer, xin, xout, pre=None):
            if pre is None:
                pre = ffn_weights(layer)
            Wgu = pre["Wgu"]
            Wd = pre["Wd"] if pre["Wd"] is not None else load_wd(layer)
            gB = A.alloc([128, D], F32)
            ssq = A.alloc([128, 32], F32)
            rst = A.alloc([128, 32], F32)
            junk = A.alloc([128, D], BF16)
            xn = [A.alloc([128, D], F32) for _ in range(4)]
            hn = [A.alloc([128, D], BF16) for _ in range(2)]
            h2T = A.alloc([128, 8, 512], BF16)
            actT = A.alloc([128, NF, 512], BF16)
            sg = [A.alloc([128, 512], F32) for _ in range(2)]
            xr = [A.alloc([128, D], F32) for _ in range(2)]
            S.dma("sp", lambda e: e.dma_start(out=gB, in_=ffn_norm[layer].partition_broadcast(128)), w=["gB"])
            S.op("dve", lambda e: e.memset(ssq, 0.0), w=["ssq%d" % c for c in range(8)])
            xin_t, xout_t = tiles(xin), tiles(xout)

            def norm_chunk(c):
                for tt in range(4):
                    T = c * 4 + tt
                    S.dma("sp", lambda e, T=T, tt=tt: e.dma_start(out=xn[tt], in_=xin_t[T]), w=["xn%d" % tt])
                    sq_accum(xn[tt], junk, ssq[:, T:T + 1], ["xn%d" % tt, "ssq%d" % c], ["ssq%d" % c])
                rstd_from(ssq[:, c * 4:(c + 1) * 4], rst[:, c * 4:(c + 1) * 4], ["ssq%d" % c], ["rst%d" % c])
                for tt in range(4):
                    T = c * 4 + tt
                    normed_transposed(xn[tt], rst[:, T:T + 1], gB, hn[tt % 2], tt % 2,
                                      h2T[:, :, tt * 128:(tt + 1) * 128],
                                      ["xn%d" % tt], ["rst%d" % c], "hn%d" % (tt % 2), "h2T")

            def gate_up(c):
                for f in range(NF):
                    gb, ub, s2 = 2 + f % 2, 4 + f % 2, f % 2
                    cg, cu = f // 4, (NF + f) // 4

                    def mm(e, f=f, gb=gb, ub=ub):
                        for bank, col in ((gb, f * 128), (ub, DFF + f * 128)):
                            for kc in range(8):
                                ins = e.matmul(ps[bank][:], lhsT=Wgu[:, kc, col:col + 128], rhs=h2T[:, kc, :],
                                               start=(kc == 0), stop=(kc == 7))
                        return ins
                    S.op("pe", mm, r=["Wgu%d" % cg, "Wgu%d" % cu, "h2T"], w=["ps%d" % gb, "ps%d" % ub])
                    S.op("act", lambda e, gb=gb, s2=s2: e.activation(out=sg[s2], in_=ps[gb][:], func=AF.Silu),
                         r=["ps%d" % gb], w=["sg%d" % s2])
                    S.op("dve", lambda e, f=f, ub=ub, s2=s2: e.tensor_tensor(out=actT[:, f, :], in0=ps[ub][:],
                                                                            in1=sg[s2], op=ALU.mult),
                         r=["ps%d" % ub, "sg%d" % s2], w=["actT"])

            def down(c):
                for tt in range(4):
                    T = c * 4 + tt
                    s2 = T % 2
                    S.dma("sp", lambda e, T=T, s2=s2: e.dma_start(out=xr[s2], in_=xin_t[T]), w=["xr%d" % s2])
                    for half in range(2):
                        bank = 6 + half

                        def mm(e, tt=tt, half=half, bank=bank):
                            for f in range(NF):
                                ins = e.matmul(ps[bank][:], lhsT=actT[:, f, tt * 128:(tt + 1) * 128],
                                               rhs=Wd[:, f, half * 512:(half + 1) * 512],
                                               start=(f == 0), stop=(f == NF - 1))
                            return ins
                        S.op("pe", mm, r=["actT", "Wd0", "Wd1"], w=["ps%d" % bank])
                        S.op("dve", lambda e, s2=s2, half=half, bank=bank: e.tensor_tensor(
                            out=xr[s2][:, half * 512:(half + 1) * 512], in0=ps[bank][:],
                            in1=xr[s2][:, half * 512:(half + 1) * 512], op=ALU.add),
                            r=["ps%d" % bank, "xr%d" % s2], w=["xr%d" % s2])
                    S.dma("pool", lambda e, T=T, s2=s2: e.dma_start(out=xout_t[T], in_=xr[s2]), r=["xr%d" % s2])

            norm_chunk(0)
            for c in range(8):
                gate_up(c)
                if c + 1 < 8:
                    norm_chunk(c + 1)
                down(c)
            S.barrier()
            A.top = const_top

        def phase_pool(xin, xout, pre_layer=None):
            pre = ffn_weights(pre_layer, want_wd=False) if pre_layer is not None else None
            Win = A.alloc([128, 8, D], BF16)
            Wg = A.alloc([128, 4, 2, 256], BF16)
            gB = A.alloc([128, D], F32)
            scB = A.alloc([128, D], F32)
            rc = A.alloc([128, 4, 16], F32)
            ssq = A.alloc([128, 32], F32)
            rst = A.alloc([128, 32], F32)
            junk = A.alloc([128, D], BF16)
            xn = [A.alloc([128, D], F32) for _ in range(4)]
            hn = [A.alloc([128, D], BF16) for _ in range(2)]
            hTc = [A.alloc([128, 8, 512], BF16) for _ in range(2)]
            uT = A.alloc([128, 8, 528], F32)
            chA = [A.alloc([128, 528], F32) for _ in range(2)]
            chB = [A.alloc([128, 528], F32) for _ in range(2)]
            yT = A.alloc([128, 8, 512], BF16)
            tmp = [A.alloc([128, 512], F32) for _ in range(2)]
            tmp16 = A.alloc([128, 16], F32)
            xr = [A.alloc([128, D], F32) for _ in range(2)]
            S.dma("pool", lambda e: e.dma_start(out=Win, in_=w_pool_in.rearrange("(kc p) c -> p kc c", p=128)),
                  w=["Win"])
            for gi in range(4):
                S.dma("pool", lambda e, gi=gi: e.dma_start(
                    out=Wg[:, gi, :, :], in_=w_pool_group[gi].rearrange("(kc p) d -> p kc d", p=128)), w=["Wg"])
            S.dma("sp", lambda e: e.dma_start(out=gB, in_=pool_norm.partition_broadcast(128)), w=["gB"])
            S.dma("sp", lambda e: e.dma_start(out=scB, in_=pool_scale.partition_broadcast(128)), w=["scB"])
            S.dma("sp", lambda e: e.dma_start(out=rc.rearrange("p a b -> p (a b)"),
                                              in_=c_rc.rearrange("a b -> (a b)").partition_broadcast(128)), w=["rc"])
            S.op("dve", lambda e: e.memset(ssq, 0.0), w=["ssq%d" % t for t in range(8)])
            S.op("dve", lambda e: e.memset(uT, 0.0), w=["uT%d" % ct for ct in range(8)])
            xin_t, xout_t = tiles(xin), tiles(xout)
            xq = [0]

            def norm_chunk(c):
                cs = c % 2
                for tt in range(4):
                    T = c * 4 + tt
                    S.dma("sp", lambda e, T=T, tt=tt: e.dma_start(out=xn[tt], in_=xin_t[T]), w=["xn%d" % tt])
                    sq_accum(xn[tt], junk, ssq[:, T:T + 1], ["xn%d" % tt, "ssq%d" % c], ["ssq%d" % c])
                rstd_from(ssq[:, c * 4:(c + 1) * 4], rst[:, c * 4:(c + 1) * 4], ["ssq%d" % c], ["rst%d" % c])
                for tt in range(4):
                    T = c * 4 + tt
                    normed_transposed(xn[tt], rst[:, T:T + 1], gB, hn[T % 2], T % 2,
                                      hTc[cs][:, :, tt * 128:(tt + 1) * 128],
                                      ["xn%d" % tt], ["rst%d" % c], "hn%d" % (T % 2), "hTc%d" % cs)

            def mix_chunk(c):
                cs = c % 2
                if c > 0:
                    S.op("dve", lambda e: e.tensor_copy(out=uT[:, :, 0:16], in_=uT[:, :, 512:528]),
                         r=["uT%d" % ct for ct in range(8)], w=["uT%d" % ct for ct in range(8)])
                for ct in range(8):
                    bank = 2 + ct % 2

                    def mm(e, ct=ct, bank=bank, cs=cs):
                        for kc in range(8):
                            ins = e.matmul(ps[bank][:], lhsT=Win[:, kc, ct * 128:(ct + 1) * 128], rhs=hTc[cs][:, kc, :],
                                           start=(kc == 0), stop=(kc == 7))
                        return ins
                    S.op("pe", mm, r=["Win", "hTc%d" % cs], w=["ps%d" % bank])
                    S.op("act", lambda e, ct=ct, bank=bank: e.activation(out=uT[:, ct, 16:528], in_=ps[bank][:],
                                                                       func=AF.Copy),
                         r=["ps%d" % bank], w=["uT%d" % ct])
                    gi = ct // 2
                    wdw = 2 ** (gi + 1)
                    cur = uT[:, ct, :]
                    cur_res = "uT%d" % ct
                    bufs = [chA[ct % 2], chB[ct % 2]]
                    bres = ["chA%d" % (ct % 2), "chB%d" % (ct % 2)]
                    for lev in range(gi + 1):
                        sh = 2 ** lev
                        lo = 2 * sh - 1
                        dstb = bufs[lev % 2]
                        S.op("dve" if gi >= 2 else "pool", lambda e, dstb=dstb, cur=cur, lo=lo, sh=sh: e.tensor_tensor(
                            out=dstb[:, lo:528], in0=cur[:, lo:528], in1=cur[:, lo - sh:528 - sh], op=ALU.add),
                            r=[cur_res], w=[bres[lev % 2]])
                        cur = dstb
                        cur_res = bres[lev % 2]
                    S.op("dve", lambda e, ct=ct, cur=cur, wdw=wdw: e.scalar_tensor_tensor(
                        out=yT[:, ct, :], in0=cur[:, 16:528], scalar=1.0 / wdw, in1=uT[:, ct, 16:528],
                        op0=ALU.mult, op1=ALU.subtract), r=[cur_res, "uT%d" % ct], w=["yT"])
                    if c == 0:
                        S.op("dve", lambda e, cur=cur, gi=gi: e.tensor_tensor(out=tmp16, in0=cur[:, 16:32],
                                                                            in1=rc[:, gi, :], op=ALU.mult),
                             r=[cur_res, "rc"], w=["tmp16"])
                        S.op("dve", lambda e, ct=ct: e.tensor_tensor(out=yT[:, ct, 0:16], in0=tmp16,
                                                                    in1=uT[:, ct, 16:32], op=ALU.subtract),
                             r=["tmp16", "uT%d" % ct, "yT"], w=["yT"])
                for tt in range(4):
                    T = c * 4 + tt
                    s2 = T % 2
                    S.dma("sp", lambda e, T=T, s2=s2: e.dma_start(out=xr[s2], in_=xin_t[T]), w=["xr%d" % s2])
                    for half in range(2):
                        bank = 4 + (T * 2 + half) % 4

                        def mm(e, tt=tt, half=half, bank=bank):
                            for gg in range(2):
                                gi = half * 2 + gg
                                for k2 in range(2):
                                    ins = e.matmul(ps[bank][:, gg * 256:(gg + 1) * 256],
                                                   lhsT=yT[:, gi * 2 + k2, tt * 128:(tt + 1) * 128],
                                                   rhs=Wg[:, gi, k2, :], start=(k2 == 0), stop=(k2 == 1))
                            return ins
                        S.op("pe", mm, r=["yT", "Wg"], w=["ps%d" % bank])
                        S.op("dve", lambda e, half=half, bank=bank: e.tensor_tensor(
                            out=tmp[half], in0=ps[bank][:], in1=scB[:, half * 512:(half + 1) * 512], op=ALU.mult),
                            r=["ps%d" % bank, "scB"], w=["tmp%d" % half])
                        S.op("pool", lambda e, s2=s2, half=half: e.tensor_tensor(
                            out=xr[s2][:, half * 512:(half + 1) * 512], in0=tmp[half],
                            in1=xr[s2][:, half * 512:(half + 1) * 512], op=ALU.add),
                            r=["tmp%d" % half, "xr%d" % s2], w=["xr%d" % s2])
                    S.dma("pool", lambda e, T=T, s2=s2: e.dma_start(out=xout_t[T], in_=xr[s2]), r=["xr%d" % s2])

            norm_chunk(0)
            for c in range(8):
                if c + 1 < 8:
                    norm_chunk(c + 1)
                mix_chunk(c)
            S.barrier()
            A.top = pre["top"] if pre else const_top
            return pre

        def phase_final(xin, xout):
            gB = A.alloc([128, D], F32)
            ssq = A.alloc([128, 32], F32)
            rst = A.alloc([128, 32], F32)
            junk = A.alloc([128, D], BF16)
            xn = [A.alloc([128, D], F32) for _ in range(4)]
            yo = [A.alloc([128, D], F32) for _ in range(3)]
            S.dma("sp", lambda e: e.dma_start(out=gB, in_=final_norm.partition_broadcast(128)), w=["gB"])
            S.op("dve", lambda e: e.memset(ssq, 0.0), w=["ssq"])
            xin_t, xout_t = tiles(xin), tiles(xout)
            for T in range(32):
                s4 = T % 4
                S.dma("sp", lambda e, T=T, s4=s4: e.dma_start(out=xn[s4], in_=xin_t[T]), w=["xn%d" % s4])
                sq_accum(xn[s4], junk, ssq[:, T:T + 1], ["xn%d" % s4, "ssq"], ["ssq"])
            rstd_from(ssq, rst, ["ssq"], ["rst"])
            for T in range(32):
                s4, s3 = T % 4, T % 3
                S.dma("sp", lambda e, T=T, s4=s4: e.dma_start(out=xn[s4], in_=xin_t[T]), w=["xn%d" % s4])
                S.op("dve", lambda e, T=T, s4=s4, s3=s3: e.scalar_tensor_tensor(
                    out=yo[s3], in0=xn[s4], scalar=rst[:, T:T + 1], in1=gB, op0=ALU.mult, op1=ALU.mult),
                    r=["xn%d" % s4, "rst", "gB"], w=["yo%d" % s3])
                S.dma("pool", lambda e, T=T, s3=s3: e.dma_start(out=xout_t[T], in_=yo[s3]), r=["yo%d" % s3])
            S.barrier()
            A.top = const_top

        def copy_out(src):
            for i in range(8):
                S.dma("sp", lambda e, i=i: e.dma_start(
                    out=out[i * 512:(i + 1) * 512, :].rearrange("(p a) d -> p (a d)", p=128),
                    in_=src[i * 512:(i + 1) * 512, :].rearrange("(p a) d -> p (a d)", p=128)))

        carry = {}
        stages = [
            ("attn", lambda: (phase_attention(x, xs[0]), carry.__setitem__("f0", phase_outproj(x, xs[0], pre_layer=0))), xs[0]),
            ("ffn0", lambda: phase_ffn(0, xs[0], xs[1], pre=carry.get("f0")), xs[1]),
            ("pool", lambda: carry.__setitem__("f1", phase_pool(xs[1], xs[2], pre_layer=1)), xs[2]),
            ("ffn1", lambda: phase_ffn(1, xs[2], xs[3], pre=carry.get("f1")), xs[3]),
            ("final", lambda: phase_final(xs[3], out), None),
        ]
        if DBG.get('nostage'):
            copy_out(x)
            stages = []
        for name, fn, res in stages:
            fn()
            if stop_after == name and res is not None:
                copy_out(res)
                break
        S.finish()
        S.emit()
    return nc


def _consts():
    ident = np.eye(128, dtype=np.float32).astype(ml_dtypes.bfloat16)
    k = np.arange(128)[:, None]
    q = np.arange(128)[None, :]
    cur = np.where(q - k >= 0, -(q - k).astype(np.float32), NEG)
    prev = np.where(k >= q, -(128 + q - k).astype(np.float32), NEG)
    base = np.concatenate([cur, prev], axis=1).astype(np.float32)
    rc = np.zeros((4, 16), np.float32)
    for gi, w in enumerate((2, 4, 8, 16)):
        rc[gi] = 1.0 / np.minimum(np.arange(1, 17), w)
    return ident, base, rc


_NC_CACHE = {}


def kernel(x, attn_norm, w_qkv, w_attn_out, pool_norm, w_pool_in, w_pool_group, pool_scale,
           ffn_norm, w_ffn_gate_up, w_ffn_down, final_norm, _stop_after=None):
    f = lambda a: np.ascontiguousarray(np.asarray(a), dtype=np.float32)
    x = f(x)
    ident, base, rc = _consts()
    shared = dict(
        attn_norm=f(attn_norm)[0], w_qkv=f(w_qkv)[0], w_attn_out=f(w_attn_out)[0],
        pool_norm=f(pool_norm)[0], w_pool_in=f(w_pool_in)[0], w_pool_group=f(w_pool_group)[0],
        pool_scale=f(pool_scale)[0], ffn_norm=f(ffn_norm), w_ffn_gate_up=f(w_ffn_gate_up),
        w_ffn_down=f(w_ffn_down), final_norm=f(final_norm),
        c_ident=ident, c_base=base, c_rc=rc,
    )
    if _stop_after not in _NC_CACHE:
        _NC_CACHE[_stop_after] = build_nc(_stop_after)
    nc = _NC_CACHE[_stop_after]
    ncores = DBG.get('ncores', 8)
    in_maps = [dict(shared, x=x[b]) for b in range(ncores)]
    res = run_bass_kernel_spmd(nc, in_maps, core_ids=list(range(ncores)))
    return np.stack([np.asarray(r["out"], dtype=np.float32) for r in res.results], axis=0)
```

```python
import contextlib
import math
import numpy as np
import ml_dtypes
import concourse.bass as bass
import concourse.mybir as mybir
from concourse.bass_utils import run_bass_kernel_spmd

F32 = mybir.dt.float32
BF16 = mybir.dt.bfloat16
AF = mybir.ActivationFunctionType
ALU = mybir.AluOpType

ENGS = ("pe", "act", "dve", "pool", "sp")
DMA_RING = 12

S_TOK = 4096
D = 1024
DFF = 2816
NF = DFF // 128
QKVW = 9216
PAIRS = ((128, 1), (512, 4), (2048, 16))
EPS = 1e-6
NEG = -1.0e5
DBG = dict(iters=24, attn_blocks=True, proj=True)
STRICT = False


class Sched:
    def __init__(self, nc):
        self.nc = nc
        self.ops = {e: [] for e in ENGS}
        self.last_w = {}
        self.readers = {}
        self.ndma = {e: 0 for e in ENGS}

    def _deps(self, eng, r, w):
        deps = set()
        for res in r:
            t = self.last_w.get(res)
            if t is not None:
                deps.add((t, "raw"))
        for res in w:
            t = self.last_w.get(res)
            if t is not None:
                deps.add((t, "waw"))
            for t in self.readers.get(res, ()):
                deps.add((t, "war"))
        out = set()
        for t, kind in deps:
            if t[0] == "eng" and t[1] == eng:
                if eng in ("pe", "sp") or (kind != "raw" and not STRICT):
                    continue
            out.add(t)
        return out

    def _commit(self, tok, r, w):
        for res in r:
            self.readers.setdefault(res, []).append(tok)
        for res in w:
            self.last_w[res] = tok
            self.readers[res] = []

    def op(self, eng, fn, r=(), w=()):
        idx = len(self.ops[eng])
        deps = self._deps(eng, r, w)
        tok = ("eng", eng, idx)
        self.ops[eng].append(dict(fn=fn, deps=deps, dma=None, sig=False))
        self._commit(tok, r, w)
        return tok

    def dma(self, eng, fn, r=(), w=()):
        deps = self._deps(eng, r, w)
        n = self.ndma[eng]
        self.ndma[eng] += 1
        tok = ("dma", eng, n)
        if n >= DMA_RING:
            deps.add(("dma", eng, n - DMA_RING))
        self.ops[eng].append(dict(fn=fn, deps=deps, dma=n, sig=False))
        self._commit(tok, r, w)
        return tok

    def _tail_tokens(self):
        toks = []
        for e in ENGS:
            for i in range(len(self.ops[e]) - 1, -1, -1):
                o = self.ops[e][i]
                if o["dma"] is None and o["fn"] is not None:
                    toks.append(("eng", e, i))
                    break
            n = self.ndma[e]
            for k in range(max(0, n - DMA_RING), n):
                toks.append(("dma", e, k))
        return toks

    def barrier(self):
        toks = self._tail_tokens()
        self.last_w = {}
        self.readers = {}
        for e in ENGS:
            deps = set(t for t in toks if not (t[0] == "eng" and t[1] == e and e in ("pe", "sp")))
            self.ops[e].append(dict(fn=None, deps=deps, dma=None, sig=False))

    def finish(self, eng="sp"):
        deps = set(t for t in self._tail_tokens() if t[0] == "dma")
        self.ops[eng].append(dict(fn=None, deps=deps, dma=None, sig=False))

    def emit(self):
        nc = self.nc
        for e in ENGS:
            for o in self.ops[e]:
                for t in o["deps"]:
                    if t[0] == "eng":
                        self.ops[t[1]][t[2]]["sig"] = True
        cnt = {}
        for e in ENGS:
            c = 0
            for i, o in enumerate(self.ops[e]):
                if o["sig"]:
                    assert o["fn"] is not None
                    c += 1
                cnt[(e, i)] = c
        with contextlib.ExitStack() as st:
            esem = {e: st.enter_context(nc.semaphore("prog_" + e)) for e in ENGS}
            dsem = {}
            for e in ENGS:
                if self.ndma[e]:
                    dsem[e] = [st.enter_context(nc.semaphore("dq_%s_%d" % (e, k)))
                               for k in range(min(DMA_RING, self.ndma[e]))]
            block = st.enter_context(nc.Block())
            ops = self.ops

            def run(e, eng):
                seen = {}
                for o in ops[e]:
                    need = {}
                    for t in o["deps"]:
                        if t[0] == "eng":
                            key = ("eng", t[1])
                            val = cnt[(t[1], t[2])]
                            sem = esem[t[1]]
                        else:
                            key = ("dma", t[1], t[2] % DMA_RING)
                            val = 16 * (t[2] // DMA_RING + 1)
                            sem = dsem[t[1]][t[2] % DMA_RING]
                        if seen.get(key, 0) >= val:
                            continue
                        if key not in need or need[key][1] < val:
                            need[key] = (sem, val)
                    for key, (sem, val) in need.items():
                        eng.wait_ge(sem, val)
                        seen[key] = val
                    if o["fn"] is None:
                        continue
                    ins = o["fn"](eng)
                    if o["dma"] is not None:
                        ins.then_inc(dsem[e][o["dma"] % DMA_RING], 16)
                    elif o["sig"]:
                        ins.then_inc(esem[e], 1)

            @block.tensor
            def _(eng):
                run("pe", eng)

            @block.scalar
            def _(eng):
                run("act", eng)

            @block.vector
            def _(eng):
                run("dve", eng)

            @block.gpsimd
            def _(eng):
                run("pool", eng)

            @block.sync
            def _(eng):
                run("sp", eng)


class Arena:
    def __init__(self, big):
        self.big = big
        self.cap = big.shape[1]
        self.top = 0

    def alloc(self, shape, dt):
        n = 1
        for s in shape[1:]:
            n *= s
        e16 = n * (2 if dt == F32 else 1)
        off = self.top
        self.top += (e16 + 15) // 16 * 16
        assert self.top <= self.cap, "SBUF arena overflow: %d > %d" % (self.top, self.cap)
        ap = self.big[0:shape[0], off:off + e16]
        if dt == F32:
            ap = ap.bitcast(F32)
        if len(shape) == 3:
            ap = ap.rearrange("p (a b) -> p a b", a=shape[1])
        elif len(shape) == 4:
            ap = ap.rearrange("p (a b c) -> p a b c", a=shape[1], b=shape[2])
        return ap


def build_nc(stop_after=None):
    nc = bass.Bass("TRN2", target_bir_lowering=False)

    def din(name, shape, dt=F32):
        return nc.dram_tensor(name, list(shape), dt, kind="ExternalInput").ap()

    x = din("x", [S_TOK, D])
    attn_norm = din("attn_norm", [D])
    w_qkv = din("w_qkv", [D, QKVW])
    w_attn_out = din("w_attn_out", [D, D])
    pool_norm = din("pool_norm", [D])
    w_pool_in = din("w_pool_in", [D, D])
    w_pool_group = din("w_pool_group", [4, 256, 256])
    pool_scale = din("pool_scale", [D])
    ffn_norm = din("ffn_norm", [2, D])
    w_gu = din("w_ffn_gate_up", [2, D, 2 * DFF])
    w_down = din("w_ffn_down", [2, DFF, D])
    final_norm = din("final_norm", [D])
    c_ident = din("c_ident", [128, 128], BF16)
    c_base = din("c_base", [128, 256])
    c_rc = din("c_rc", [4, 16])
    out = nc.dram_tensor("out", [S_TOK, D], F32, kind="ExternalOutput").ap()

    xs = [nc.dram_tensor("xs%d" % i, [S_TOK, D], F32).ap() for i in range(4)]
    OTs = nc.dram_tensor("ots", [D, S_TOK], BF16).ap()

    st = contextlib.ExitStack()
    with st:
        big = st.enter_context(nc.sbuf_tensor("big", [128, 105984], BF16))
        ps = [st.enter_context(nc.psum_tensor("ps%d" % i, [128, 512], F32)) for i in range(8)]
        S = Sched(nc)
        A = Arena(big)

        def tiles(ap):
            return ap.rearrange("(n p) d -> n p d", p=128)

        idt = A.alloc([128, 128], BF16)
        ones = A.alloc([128, 128], BF16)
        base2 = A.alloc([128, 512], F32)
        S.dma("sp", lambda e: e.dma_start(out=idt, in_=c_ident), w=["idt"])
        S.dma("sp", lambda e: e.dma_start(out=base2[:, 0:256], in_=c_base), w=["base"])
        S.dma("sp", lambda e: e.dma_start(out=base2[:, 256:512], in_=c_base), w=["base"])
        S.op("dve", lambda e: e.memset(ones, 1.0), w=["ones"])
        const_top = A.top

        def pT(b):
            return ps[b][:].bitcast(BF16)

        def sq_accum(xt_ap, junk, ssq_col, rx, wss):
            S.op("act", lambda e: e.activation(out=junk, in_=xt_ap, func=AF.Square, accum_out=ssq_col),
                 r=rx, w=["junk"] + wss)

        def rstd_from(ssq_ap, rst_ap, rr, ww):
            S.op("dve", lambda e: e.tensor_scalar(out=rst_ap, in0=ssq_ap, scalar1=1.0 / D, scalar2=EPS,
                                                 op0=ALU.mult, op1=ALU.add), r=rr, w=ww)
            S.op("act", lambda e: e.activation(out=rst_ap, in_=rst_ap, func=AF.Sqrt), r=ww, w=ww)
            S.op("dve", lambda e: e.reciprocal(out=rst_ap, in_=rst_ap), r=ww, w=ww)

        def normed_transposed(xt_ap, rst_col, gB, hn_ap, bank, dst_ap, rx, rrst, hn_res, dst_res):
            S.op("dve", lambda e: e.scalar_tensor_tensor(out=hn_ap, in0=xt_ap, scalar=rst_col, in1=gB,
                                                        op0=ALU.mult, op1=ALU.mult),
                 r=rx + rrst + ["gB"], w=[hn_res])

            def tr(e):
                for kc in range(8):
                    ins = e.transpose(out=pT(bank)[:, kc * 128:(kc + 1) * 128],
                                      in_=hn_ap[:, kc * 128:(kc + 1) * 128], identity=idt)
                return ins
            S.op("pe", tr, r=[hn_res, "idt"], w=["ps%d" % bank])
            S.op("act", lambda e: e.activation(out=dst_ap,
                                               in_=pT(bank).rearrange("p (a b) -> p a b", a=8),
                                               func=AF.Copy),
                 r=["ps%d" % bank], w=[dst_res])

        def phase_attention(xin, xout):
            hT = A.alloc([128, 8, S_TOK], BF16)
            gB = A.alloc([128, D], F32)
            ssq = A.alloc([128, 32], F32)
            rst = A.alloc([128, 32], F32)
            junk = A.alloc([128, D], BF16)
            p0_top = A.top
            xt = [A.alloc([128, D], F32) for _ in range(6)]
            hn = [A.alloc([128, D], BF16) for _ in range(2)]
            S.dma("sp", lambda e: e.dma_start(out=gB, in_=attn_norm.partition_broadcast(128)), w=["gB"])
            S.op("dve", lambda e: e.memset(ssq, 0.0), w=["ssq"])
            xin_t = tiles(xin)
            for tt in range(32):
                s3 = tt % 6
                S.dma("sp", lambda e, tt=tt, s3=s3: e.dma_start(out=xt[s3], in_=xin_t[tt]), w=["xt%d" % s3])
                sq_accum(xt[s3], junk, ssq[:, tt:tt + 1], ["xt%d" % s3, "ssq"], ["ssq"])
            rstd_from(ssq, rst, ["ssq"], ["rst"])
            for tt in range(32):
                s3, s2 = (tt + 32) % 6, tt % 2
                S.dma("sp", lambda e, tt=tt, s3=s3: e.dma_start(out=xt[s3], in_=xin_t[tt]), w=["xt%d" % s3])
                normed_transposed(xt[s3], rst[:, tt:tt + 1], gB, hn[s2], tt % 2,
                                  hT[:, :, tt * 128:(tt + 1) * 128],
                                  ["xt%d" % s3], ["rst"], "hn%d" % s2, "hT")
            S.barrier()
            A.top = p0_top
            XT = [A.alloc([128, S_TOK], BF16) for _ in range(3)]
            Vtok = A.alloc([128, 32, 128], BF16)
            PT = A.alloc([128, 32, 256], BF16)
            accU = A.alloc([128, S_TOK], F32)
            accD = A.alloc([128, S_TOK], F32)
            wq = [A.alloc([128, 8, 384], BF16) for _ in range(2)]
            OTh = A.alloc([128, S_TOK], BF16)
            Tb = [A.alloc([128, 512], F32) for _ in range(3)]
            XTn = ["QT", "KT", "VT"]

            iters = [(h, g) for h in range(8) for g in range(3)]

            def load_w(n):
                h, g = iters[n]
                slot = n % 2
                for part in range(3):
                    c0 = g * 3072 + part * 1024 + h * 128
                    S.dma("pool", lambda e, c0=c0, part=part, slot=slot: e.dma_start(
                        out=wq[slot][:, :, part * 128:(part + 1) * 128],
                        in_=w_qkv[:, c0:c0 + 128].rearrange("(kc p) c -> p kc c", p=128)),
                        w=["wq%d" % slot])

            if DBG['iters'] > 0:
                load_w(0)
            evq = [0]

            def evac(dst, src, rr, ww):
                evq[0] += 1
                if evq[0] % 2:
                    S.op("act", lambda e: e.activation(out=dst, in_=src, func=AF.Copy), r=rr, w=ww)
                else:
                    S.op("dve", lambda e: e.tensor_copy(out=dst, in_=src), r=rr, w=ww)

            inv_sqrt_d = 1.0 / math.sqrt(128.0)
            def do_iter(n, h, g):
                window, dil = PAIRS[g]
                L = S_TOK // dil
                nb = L // 128
                slot = n % 2
                if n + 1 < min(len(iters), DBG['iters']):
                    load_w(n + 1)
                pb = 0
                for part in range(3):
                    for tc in range(8):
                        bank = pb % 2
                        pb += 1

                        def mm(e, part=part, tc=tc, bank=bank, slot=slot):
                            for kc in range(8):
                                ins = e.matmul(ps[bank][:], lhsT=wq[slot][:, kc, part * 128:(part + 1) * 128],
                                               rhs=hT[:, kc, tc * 512:(tc + 1) * 512],
                                               start=(kc == 0), stop=(kc == 7))
                            return ins
                        S.op("pe", mm, r=["wq%d" % slot, "hT"], w=["ps%d" % bank])
                        jn = 512 // dil
                        j0 = tc * jn
                        dst = XT[part].rearrange("p (r j) -> p r j", r=dil)[:, :, j0:j0 + jn]
                        src = ps[bank][:].rearrange("p (j r) -> p r j", r=dil)
                        evac(dst, src, ["ps%d" % bank], [XTn[part]])
                for q4 in range(4):
                    bank = pb % 2
                    pb += 1

                    def trv(e, q4=q4, bank=bank):
                        for i in range(8):
                            b = q4 * 8 + i
                            ins = e.transpose(out=pT(bank)[:, i * 128:(i + 1) * 128],
                                              in_=XT[2][:, b * 128:(b + 1) * 128], identity=idt)
                        return ins
                    S.op("pe", trv, r=["VT", "idt"], w=["ps%d" % bank])
                    evac(Vtok[:, q4 * 8:(q4 + 1) * 8, :], pT(bank).rearrange("p (a b) -> p a b", a=8),
                         ["ps%d" % bank], ["Vtok"])
                slope = 2.0 ** (-8.0 * (g * 8 + h + 1) / 24.0)
                sc = slope * dil * math.sqrt(128.0)
                LAG = 5

                def s_pair(p):
                    b = 2 * p
                    last = ((b + 1) % nb) == nb - 1
                    W = 128 if last else 256
                    WW = 256 + W
                    bank = 2 + p % 4
                    tb = p % 3

                    def mms(e):
                        e.matmul(ps[bank][:, 0:256], lhsT=XT[1][:, b * 128:(b + 1) * 128],
                                 rhs=XT[0][:, b * 128:b * 128 + 256], start=True, stop=True)
                        return e.matmul(ps[bank][:, 256:256 + W], lhsT=XT[1][:, (b + 1) * 128:(b + 2) * 128],
                                        rhs=XT[0][:, (b + 1) * 128:(b + 1) * 128 + W], start=True, stop=True)
                    S.op("pe", mms, r=["QT", "KT"], w=["ps%d" % bank])
                    S.op("dve", lambda e: e.scalar_tensor_tensor(out=Tb[tb][:, :WW], in0=base2[:, :WW], scalar=sc,
                                                                in1=ps[bank][:, :WW],
                                                                op0=ALU.mult, op1=ALU.add),
                         r=["ps%d" % bank, "base"], w=["Tb%d" % tb])
                    S.op("act", lambda e: e.activation(
                        out=PT[:, b:b + 2, :].rearrange("p a b -> p (a b)")[:, :WW], in_=Tb[tb][:, :WW],
                        func=AF.Exp, scale=inv_sqrt_d),
                        r=["Tb%d" % tb], w=["PT%d" % b, "PT%d" % (b + 1)])

                def pv_group(b0, gi):
                    bank = 6 + gi % 2
                    rr = set(["Vtok", "ones"])

                    def mmpv(e):
                        ins = None
                        for which, lhs_of in ((0, lambda bb: Vtok[:, bb, :]), (1, lambda bb: ones)):
                            for i in range(2):
                                b = b0 + i
                                o = ps[bank][:, which * 256 + i * 128: which * 256 + (i + 1) * 128]
                                if b % nb > 0:
                                    e.matmul(o, lhsT=lhs_of(b - 1), rhs=PT[:, b - 1, 128:256], start=True, stop=False)
                                    ins = e.matmul(o, lhsT=lhs_of(b), rhs=PT[:, b, 0:128], start=False, stop=True)
                                else:
                                    ins = e.matmul(o, lhsT=lhs_of(b), rhs=PT[:, b, 0:128], start=True, stop=True)
                        return ins
                    for b in (b0 - 1, b0, b0 + 1):
                        if b >= 0:
                            rr.add("PT%d" % b)
                    S.op("pe", mmpv, r=sorted(rr), w=["psV%d" % (gi % 2)])
                    r_ = b0 // nb
                    n0 = b0 % nb
                    j0 = n0 * 128
                    for which, acc, nm in ((0, accU, "accU"), (1, accD, "accD")):
                        dst = acc.rearrange("p (j r) -> p r j", r=dil)[:, r_, j0:j0 + 256]
                        src = ps[bank][:, which * 256:(which + 1) * 256]
                        if g == 0:
                            S.op("act", lambda e, dst=dst, src=src: e.activation(out=dst, in_=src, func=AF.Copy),
                                 r=["psV%d" % (gi % 2)], w=[nm + "_g0", nm + "_g2"])
                        else:
                            S.op("dve", lambda e, dst=dst, src=src: e.tensor_tensor(out=dst, in0=src, in1=dst, op=ALU.add),
                                 r=["psV%d" % (gi % 2), nm + "_g%d" % (g - 1)], w=[nm + "_g%d" % g])

                LAGP = 2
                for p in range(16 + LAGP):
                    if p < 16:
                        s_pair(p)
                    if p - LAGP >= 0:
                        pv_group(2 * (p - LAGP), p - LAGP)
                if g == 2:
                    S.op("dve", lambda e: e.reciprocal(out=accD, in_=accD), r=["accD_g2"], w=["accD_g2"])
                    S.op("dve", lambda e: e.tensor_tensor(out=OTh, in0=accU, in1=accD, op=ALU.mult),
                         r=["accU_g2", "accD_g2"], w=["OTh"])
                    S.dma("sp", lambda e, h=h: e.dma_start(out=OTs[h * 128:(h + 1) * 128, :], in_=OTh), r=["OTh"])
            for n_, (h_, g_) in enumerate(iters[:DBG['iters']]):
                do_iter(n_, h_, g_)
            S.barrier()
            A.top = const_top
            return

        def ffn_weights(layer, want_wd=True):
            Wgu = A.alloc([128, 8, 2 * DFF], BF16)
            wguv = w_gu[layer].rearrange("(kc p) c -> p kc c", p=128)
            for c in (0, 5, 6, 1, 7, 2, 8, 3, 9, 4, 10):
                S.dma("pool", lambda e, c=c: e.dma_start(out=Wgu[:, :, c * 512:(c + 1) * 512],
                                                        in_=wguv[:, :, c * 512:(c + 1) * 512]), w=["Wgu%d" % c])
            Wd = None
            if want_wd:
                Wd = load_wd(layer)
            return dict(Wgu=Wgu, Wd=Wd, top=A.top)

        def load_wd(layer):
            Wd = A.alloc([128, NF, D], BF16)
            wdv = w_down[layer].rearrange("(f p) c -> p f c", p=128)
            for hf in range(2):
                S.dma("pool", lambda e, hf=hf: e.dma_start(out=Wd[:, hf * 11:(hf + 1) * 11, :],
                                                          in_=wdv[:, hf * 11:(hf + 1) * 11, :]), w=["Wd%d" % hf])
            return Wd

        def phase_outproj(xin, xout, pre_layer=None):
            pre = ffn_weights(pre_layer) if pre_layer is not None else None
            Wo = A.alloc([128, 8, D], BF16)
            OTc = [A.alloc([128, 8, 512], BF16) for _ in range(2)]
            xr = [A.alloc([128, D], F32) for _ in range(4)]
            S.dma("pool", lambda e: e.dma_start(out=Wo, in_=w_attn_out.rearrange("(kc p) c -> p kc c", p=128)),
                  w=["Wo"])
            OTv = OTs.rearrange("(kc p) t -> p kc t", p=128)
            xin_t, xout_t = tiles(xin), tiles(xout)
            pb = 0
            for tc in range(8):
                cs = tc % 2
                S.dma("sp", lambda e, tc=tc, cs=cs: e.dma_start(out=OTc[cs], in_=OTv[:, :, tc * 512:(tc + 1) * 512]),
                      w=["OTc%d" % cs])
                for tt in range(4):
                    T = tc * 4 + tt
                    s2 = T % 4
                    S.dma("sp", lambda e, T=T, s2=s2: e.dma_start(out=xr[s2], in_=xin_t[T]), w=["xr%d" % s2])
                    for half in range(2):
                        bank = pb % 4
                        pb += 1

                        def mm(e, cs=cs, tt=tt, half=half, bank=bank):
                            for kc in range(8):
                                ins = e.matmul(ps[bank][:], lhsT=OTc[cs][:, kc, tt * 128:(tt + 1) * 128],
                                               rhs=Wo[:, kc, half * 512:(half + 1) * 512],
                                               start=(kc == 0), stop=(kc == 7))
                            return ins
                        S.op("pe", mm, r=["OTc%d" % cs, "Wo"], w=["ps%d" % bank])
                        S.op("dve", lambda e, s2=s2, half=half, bank=bank: e.tensor_tensor(
                            out=xr[s2][:, half * 512:(half + 1) * 512], in0=ps[bank][:],
                            in1=xr[s2][:, half * 512:(half + 1) * 512], op=ALU.add),
                            r=["ps%d" % bank, "xr%d" % s2], w=["xr%d" % s2])
                    S.dma("pool", lambda e, T=T, s2=s2: e.dma_start(out=xout_t[T], in_=xr[s2]), r=["xr%d" % s2])
            S.barrier()
            A.top = pre["top"] if pre else const_top
            return pre

        def phase_ffn(layer, xin, xout, pre=None):
            if pre is None:
                pre = ffn_weights(layer)
            Wgu = pre["Wgu"]
            Wd = pre["Wd"] if pre["Wd"] is not None else load_wd(layer)
            gB = A.alloc([128, D], F32)
            ssq = A.alloc([128, 32], F32)
            rst = A.alloc([128, 32], F32)
            junk = A.alloc([128, D], BF16)
            xn = [A.alloc([128, D], F32) for _ in range(4)]
            hn = [A.alloc([128, D], BF16) for _ in range(2)]
            h2T = A.alloc([128, 8, 512], BF16)
            actT = A.alloc([128, NF, 512], BF16)
            sg = [A.alloc([128, 512], F32) for _ in range(2)]
            xr = [A.alloc([128, D], F32) for _ in range(2)]
            S.dma("sp", lambda e: e.dma_start(out=gB, in_=ffn_norm[layer].partition_broadcast(128)), w=["gB"])
            S.op("dve", lambda e: e.memset(ssq, 0.0), w=["ssq%d" % c for c in range(8)])
            xin_t, xout_t = tiles(xin), tiles(xout)

            def norm_chunk(c):
                for tt in range(4):
                    T = c * 4 + tt
                    S.dma("sp", lambda e, T=T, tt=tt: e.dma_start(out=xn[tt], in_=xin_t[T]), w=["xn%d" % tt])
                    sq_accum(xn[tt], junk, ssq[:, T:T + 1], ["xn%d" % tt, "ssq%d" % c], ["ssq%d" % c])
                rstd_from(ssq[:, c * 4:(c + 1) * 4], rst[:, c * 4:(c + 1) * 4], ["ssq%d" % c], ["rst%d" % c])
                for tt in range(4):
                    T = c * 4 + tt
                    normed_transposed(xn[tt], rst[:, T:T + 1], gB, hn[tt % 2], tt % 2,
                                      h2T[:, :, tt * 128:(tt + 1) * 128],
                                      ["xn%d" % tt], ["rst%d" % c], "hn%d" % (tt % 2), "h2T")

            def gate_up(c):
                for f in range(NF):
                    gb, ub, s2 = 2 + f % 2, 4 + f % 2, f % 2
                    cg, cu = f // 4, (NF + f) // 4

                    def mm(e, f=f, gb=gb, ub=ub):
                        for bank, col in ((gb, f * 128), (ub, DFF + f * 128)):
                            for kc in range(8):
                                ins = e.matmul(ps[bank][:], lhsT=Wgu[:, kc, col:col + 128], rhs=h2T[:, kc, :],
                                               start=(kc == 0), stop=(kc == 7))
                        return ins
                    S.op("pe", mm, r=["Wgu%d" % cg, "Wgu%d" % cu, "h2T"], w=["ps%d" % gb, "ps%d" % ub])
                    S.op("act", lambda e, gb=gb, s2=s2: e.activation(out=sg[s2], in_=ps[gb][:], func=AF.Silu),
                         r=["ps%d" % gb], w=["sg%d" % s2])
                    S.op("dve", lambda e, f=f, ub=ub, s2=s2: e.tensor_tensor(out=actT[:, f, :], in0=ps[ub][:],
                                                                            in1=sg[s2], op=ALU.mult),
                         r=["ps%d" % ub, "sg%d" % s2], w=["actT"])

            def down(c):
                for tt in range(4):
                    T = c * 4 + tt
                    s2 = T % 2
                    S.dma("sp", lambda e, T=T, s2=s2: e.dma_start(out=xr[s2], in_=xin_t[T]), w=["xr%d" % s2])
                    for half in range(2):
                        bank = 6 + half

                        def mm(e, tt=tt, half=half, bank=bank):
                            for f in range(NF):
                                ins = e.matmul(ps[bank][:], lhsT=actT[:, f, tt * 128:(tt + 1) * 128],
                                               rhs=Wd[:, f, half * 512:(half + 1) * 512],
                                               start=(f == 0), stop=(f == NF - 1))
                            return ins
                        S.op("pe", mm, r=["actT", "Wd0", "Wd1"], w=["ps%d" % bank])
                        S.op("dve", lambda e, s2=s2, half=half, bank=bank: e.tensor_tensor(
                            out=xr[s2][:, half * 512:(half + 1) * 512], in0=ps[bank][:],
                            in1=xr[s2][:, half * 512:(half + 1) * 512], op=ALU.add),
                            r=["ps%d" % bank, "xr%d" % s2], w=["xr%d" % s2])
                    S.dma("pool", lambda e, T=T, s2=s2: e.dma_start(out=xout_t[T], in_=xr[s2]), r=["xr%d" % s2])

            norm_chunk(0)
            for c in range(8):
                gate_up(c)
                if c + 1 < 8:
                    norm_chunk(c + 1)
                down(c)
            S.barrier()
            A.top = const_top

        def phase_pool(xin, xout, pre_layer=None):
            pre = ffn_weights(pre_layer, want_wd=False) if pre_layer is not None else None
            Win = A.alloc([128, 8, D], BF16)
            Wg = A.alloc([128, 4, 2, 256], BF16)
            gB = A.alloc([128, D], F32)
            scB = A.alloc([128, D], F32)
            rc = A.alloc([128, 4, 16], F32)
            ssq = A.alloc([128, 32], F32)
            rst = A.alloc([128, 32], F32)
            junk = A.alloc([128, D], BF16)
            xn = [A.alloc([128, D], F32) for _ in range(4)]
            hn = [A.alloc([128, D], BF16) for _ in range(2)]
            hTc = [A.alloc([128, 8, 512], BF16) for _ in range(2)]
            uT = A.alloc([128, 8, 528], F32)
            chA = [A.alloc([128, 528], F32) for _ in range(2)]
            chB = [A.alloc([128, 528], F32) for _ in range(2)]
            yT = A.alloc([128, 8, 512], BF16)
            tmp = [A.alloc([128, 512], F32) for _ in range(2)]
            tmp16 = A.alloc([128, 16], F32)
            xr = [A.alloc([128, D], F32) for _ in range(3)]
            S.dma("pool", lambda e: e.dma_start(out=Win, in_=w_pool_in.rearrange("(kc p) c -> p kc c", p=128)),
                  w=["Win"])
            for gi in range(4):
                S.dma("pool", lambda e, gi=gi: e.dma_start(
                    out=Wg[:, gi, :, :], in_=w_pool_group[gi].rearrange("(kc p) d -> p kc d", p=128)), w=["Wg"])
            S.dma("sp", lambda e: e.dma_start(out=gB, in_=pool_norm.partition_broadcast(128)), w=["gB"])
            S.dma("sp", lambda e: e.dma_start(out=scB, in_=pool_scale.partition_broadcast(128)), w=["scB"])
            S.dma("sp", lambda e: e.dma_start(out=rc.rearrange("p a b -> p (a b)"),
                                              in_=c_rc.rearrange("a b -> (a b)").partition_broadcast(128)), w=["rc"])
            S.op("dve", lambda e: e.memset(ssq, 0.0), w=["ssq%d" % t for t in range(8)])
            S.op("dve", lambda e: e.memset(uT, 0.0), w=["uT%d" % ct for ct in range(8)])
            xin_t, xout_t = tiles(xin), tiles(xout)
            xq = [0]

            def norm_chunk(c):
                cs = c % 2
                for tt in range(4):
                    T = c * 4 + tt
                    S.dma("sp", lambda e, T=T, tt=tt: e.dma_start(out=xn[tt], in_=xin_t[T]), w=["xn%d" % tt])
                    sq_accum(xn[tt], junk, ssq[:, T:T + 1], ["xn%d" % tt, "ssq%d" % c], ["ssq%d" % c])
                rstd_from(ssq[:, c * 4:(c + 1) * 4], rst[:, c * 4:(c + 1) * 4], ["ssq%d" % c], ["rst%d" % c])
                for tt in range(4):
                    T = c * 4 + tt
                    normed_transposed(xn[tt], rst[:, T:T + 1], gB, hn[T % 2], T % 2,
                                      hTc[cs][:, :, tt * 128:(tt + 1) * 128],
                                      ["xn%d" % tt], ["rst%d" % c], "hn%d" % (T % 2), "hTc%d" % cs)

            def mix_chunk(c):
                cs = c % 2
                if c > 0:
                    S.op("dve", lambda e: e.tensor_copy(out=uT[:, :, 0:16], in_=uT[:, :, 512:528]),
                         r=["uT%d" % ct for ct in range(8)], w=["uT%d" % ct for ct in range(8)])
                for ct in range(8):
                    bank = 2 + ct % 2

                    def mm(e, ct=ct, bank=bank, cs=cs):
                        for kc in range(8):
                            ins = e.matmul(ps[bank][:], lhsT=Win[:, kc, ct * 128:(ct + 1) * 128], rhs=hTc[cs][:, kc, :],
                                           start=(kc == 0), stop=(kc == 7))
                        return ins
                    S.op("pe", mm, r=["Win", "hTc%d" % cs], w=["ps%d" % bank])
                    S.op("act", lambda e, ct=ct, bank=bank: e.activation(out=uT[:, ct, 16:528], in_=ps[bank][:],
                                                                       func=AF.Copy),
                         r=["ps%d" % bank], w=["uT%d" % ct])
                    gi = ct // 2
                    wdw = 2 ** (gi + 1)
                    cur = uT[:, ct, :]
                    cur_res = "uT%d" % ct
                    bufs = [chA[ct % 2], chB[ct % 2]]
                    bres = ["chA%d" % (ct % 2), "chB%d" % (ct % 2)]
                    for lev in range(gi + 1):
                        sh = 2 ** lev
                        lo = 2 * sh - 1
                        dstb = bufs[lev % 2]
                        S.op("dve" if gi >= 2 else "pool", lambda e, dstb=dstb, cur=cur, lo=lo, sh=sh: e.tensor_tensor(
                            out=dstb[:, lo:528], in0=cur[:, lo:528], in1=cur[:, lo - sh:528 - sh], op=ALU.add),
                            r=[cur_res], w=[bres[lev % 2]])
                        cur = dstb
                        cur_res = bres[lev % 2]
                    S.op("dve", lambda e, ct=ct, cur=cur, wdw=wdw: e.scalar_tensor_tensor(
                        out=yT[:, ct, :], in0=cur[:, 16:528], scalar=1.0 / wdw, in1=uT[:, ct, 16:528],
                        op0=ALU.mult, op1=ALU.subtract), r=[cur_res, "uT%d" % ct], w=["yT"])
                    if c == 0:
                        S.op("dve", lambda e, cur=cur, gi=gi: e.tensor_tensor(out=tmp16, in0=cur[:, 16:32],
                                                                            in1=rc[:, gi, :], op=ALU.mult),
                             r=[cur_res, "rc"], w=["tmp16"])
                        S.op("dve", lambda e, ct=ct: e.tensor_tensor(out=yT[:, ct, 0:16], in0=tmp16,
                                                                    in1=uT[:, ct, 16:32], op=ALU.subtract),
                             r=["tmp16", "uT%d" % ct, "yT"], w=["yT"])
                for tt in range(4):
                    T = c * 4 + tt
                    s2 = T % 3
                    S.dma("sp", lambda e, T=T, s2=s2: e.dma_start(out=xr[s2], in_=xin_t[T]), w=["xr%d" % s2])
                    for half in range(2):
                        bank = 4 + (T * 2 + half) % 4

                        def mm(e, tt=tt, half=half, bank=bank):
                            for gg in range(2):
                                gi = half * 2 + gg
                                for k2 in range(2):
                                    ins = e.matmul(ps[bank][:, gg * 256:(gg + 1) * 256],
                                                   lhsT=yT[:, gi * 2 + k2, tt * 128:(tt + 1) * 128],
                                                   rhs=Wg[:, gi, k2, :], start=(k2 == 0), stop=(k2 == 1))
                            return ins
                        S.op("pe", mm, r=["yT", "Wg"], w=["ps%d" % bank])
                        S.op("dve", lambda e, half=half, bank=bank: e.tensor_tensor(
                            out=tmp[half], in0=ps[bank][:], in1=scB[:, half * 512:(half + 1) * 512], op=ALU.mult),
                            r=["ps%d" % bank, "scB"], w=["tmp%d" % half])
                        S.op("pool", lambda e, s2=s2, half=half: e.tensor_tensor(
                            out=xr[s2][:, half * 512:(half + 1) * 512], in0=tmp[half],
                            in1=xr[s2][:, half * 512:(half + 1) * 512], op=ALU.add),
                            r=["tmp%d" % half, "xr%d" % s2], w=["xr%d" % s2])
                    S.dma("pool", lambda e, T=T, s2=s2: e.dma_start(out=xout_t[T], in_=xr[s2]), r=["xr%d" % s2])

            norm_chunk(0)
            for c in range(8):
                if c + 1 < 8:
                    norm_chunk(c + 1)
                mix_chunk(c)
            S.barrier()
            A.top = pre["top"] if pre else const_top
            return pre

        def phase_final(xin, xout):
            gB = A.alloc([128, D], F32)
            ssq = A.alloc([128, 32], F32)
            rst = A.alloc([128, 32], F32)
            junk = A.alloc([128, D], BF16)
            xn = [A.alloc([128, D], F32) for _ in range(8)]
            yo = [A.alloc([128, D], F32) for _ in range(6)]
            S.dma("sp", lambda e: e.dma_start(out=gB, in_=final_norm.partition_broadcast(128)), w=["gB"])
            S.op("dve", lambda e: e.memset(ssq, 0.0), w=["ssq"])
            xin_t, xout_t = tiles(xin), tiles(xout)
            for T in range(32):
                s4 = T % 8
                S.dma("sp", lambda e, T=T, s4=s4: e.dma_start(out=xn[s4], in_=xin_t[T]), w=["xn%d" % s4])
                sq_accum(xn[s4], junk, ssq[:, T:T + 1], ["xn%d" % s4, "ssq"], ["ssq"])
            rstd_from(ssq, rst, ["ssq"], ["rst"])
            for T in range(32):
                s4, s3 = T % 8, T % 6
                S.dma("sp", lambda e, T=T, s4=s4: e.dma_start(out=xn[s4], in_=xin_t[T]), w=["xn%d" % s4])
                S.op("dve", lambda e, T=T, s4=s4, s3=s3: e.scalar_tensor_tensor(
                    out=yo[s3], in0=xn[s4], scalar=rst[:, T:T + 1], in1=gB, op0=ALU.mult, op1=ALU.mult),
                    r=["xn%d" % s4, "rst", "gB"], w=["yo%d" % s3])
                S.dma("pool", lambda e, T=T, s3=s3: e.dma_start(out=xout_t[T], in_=yo[s3]), r=["yo%d" % s3])
            S.barrier()
            A.top = const_top

        def copy_out(src):
            for i in range(8):
                S.dma("sp", lambda e, i=i: e.dma_start(
                    out=out[i * 512:(i + 1) * 512, :].rearrange("(p a) d -> p (a d)", p=128),
                    in_=src[i * 512:(i + 1) * 512, :].rearrange("(p a) d -> p (a d)", p=128)))

        carry = {}
        stages = [
            ("attn", lambda: (phase_attention(x, xs[0]), carry.__setitem__("f0", phase_outproj(x, xs[0], pre_layer=0))), xs[0]),
            ("ffn0", lambda: phase_ffn(0, xs[0], xs[1], pre=carry.get("f0")), xs[1]),
            ("pool", lambda: carry.__setitem__("f1", phase_pool(xs[1], xs[2], pre_layer=1)), xs[2]),
            ("ffn1", lambda: phase_ffn(1, xs[2], xs[3], pre=carry.get("f1")), xs[3]),
            ("final", lambda: phase_final(xs[3], out), None),
        ]
        if DBG.get('nostage'):
            copy_out(x)
            stages = []
        for name, fn, res in stages:
            fn()
            if stop_after == name and res is not None:
                copy_out(res)
                break
        S.finish()
        S.emit()
    return nc


def _consts():
    ident = np.eye(128, dtype=np.float32).astype(ml_dtypes.bfloat16)
    k = np.arange(128)[:, None]
    q = np.arange(128)[None, :]
    cur = np.where(q - k >= 0, -(q - k).astype(np.float32), NEG)
    prev = np.where(k >= q, -(128 + q - k).astype(np.float32), NEG)
    base = np.concatenate([cur, prev], axis=1).astype(np.float32)
    rc = np.zeros((4, 16), np.float32)
    for gi, w in enumerate((2, 4, 8, 16)):
        rc[gi] = 1.0 / np.minimum(np.arange(1, 17), w)
    return ident, base, rc


_NC_CACHE = {}


def kernel(x, attn_norm, w_qkv, w_attn_out, pool_norm, w_pool_in, w_pool_group, pool_scale,
           ffn_norm, w_ffn_gate_up, w_ffn_down, final_norm, _stop_after=None):
    f = lambda a: np.ascontiguousarray(np.asarray(a), dtype=np.float32)
    x = f(x)
    ident, base, rc = _consts()
    shared = dict(
        attn_norm=f(attn_norm)[0], w_qkv=f(w_qkv)[0], w_attn_out=f(w_attn_out)[0],
        pool_norm=f(pool_norm)[0], w_pool_in=f(w_pool_in)[0], w_pool_group=f(w_pool_group)[0],
        pool_scale=f(pool_scale)[0], ffn_norm=f(ffn_norm), w_ffn_gate_up=f(w_ffn_gate_up),
        w_ffn_down=f(w_ffn_down), final_norm=f(final_norm),
        c_ident=ident, c_base=base, c_rc=rc,
    )
    if _stop_after not in _NC_CACHE:
        _NC_CACHE[_stop_after] = build_nc(_stop_after)
    nc = _NC_CACHE[_stop_after]
    ncores = DBG.get('ncores', 8)
    in_maps = [dict(shared, x=x[b]) for b in range(ncores)]
    res = run_bass_kernel_spmd(nc, in_maps, core_ids=list(range(ncores)))
    return np.stack([np.asarray(r["out"], dtype=np.float32) for r in res.results], axis=0)
```

```python
import contextlib
import math
import numpy as np
import ml_dtypes
import concourse.bass as bass
import concourse.mybir as mybir
from concourse.bass_utils import run_bass_kernel_spmd

F32 = mybir.dt.float32
BF16 = mybir.dt.bfloat16
AF = mybir.ActivationFunctionType
ALU = mybir.AluOpType

ENGS = ("pe", "act", "dve", "pool", "sp")
DMA_RING = 12

S_TOK = 4096
D = 1024
DFF = 2816
NF = DFF // 128
QKVW = 9216
PAIRS = ((128, 1), (512, 4), (2048, 16))
EPS = 1e-6
NEG = -1.0e5
DBG = dict(iters=24, attn_blocks=True, proj=True)
STRICT = False


class Sched:
    def __init__(self, nc):
        self.nc = nc
        self.ops = {e: [] for e in ENGS}
        self.last_w = {}
        self.readers = {}
        self.ndma = {e: 0 for e in ENGS}

    def _deps(self, eng, r, w):
        deps = set()
        for res in r:
            t = self.last_w.get(res)
            if t is not None:
                deps.add((t, "raw"))
        for res in w:
            t = self.last_w.get(res)
            if t is not None:
                deps.add((t, "waw"))
            for t in self.readers.get(res, ()):
                deps.add((t, "war"))
        out = set()
        for t, kind in deps:
            if t[0] == "eng" and t[1] == eng:
                if eng in ("pe", "sp") or (kind != "raw" and not STRICT):
                    continue
            out.add(t)
        return out

    def _commit(self, tok, r, w):
        for res in r:
            self.readers.setdefault(res, []).append(tok)
        for res in w:
            self.last_w[res] = tok
            self.readers[res] = []

    def op(self, eng, fn, r=(), w=()):
        idx = len(self.ops[eng])
        deps = self._deps(eng, r, w)
        tok = ("eng", eng, idx)
        self.ops[eng].append(dict(fn=fn, deps=deps, dma=None, sig=False))
        self._commit(tok, r, w)
        return tok

    def dma(self, eng, fn, r=(), w=()):
        deps = self._deps(eng, r, w)
        n = self.ndma[eng]
        self.ndma[eng] += 1
        tok = ("dma", eng, n)
        if n >= DMA_RING:
            deps.add(("dma", eng, n - DMA_RING))
        self.ops[eng].append(dict(fn=fn, deps=deps, dma=n, sig=False))
        self._commit(tok, r, w)
        return tok

    def _tail_tokens(self):
        toks = []
        for e in ENGS:
            for i in range(len(self.ops[e]) - 1, -1, -1):
                o = self.ops[e][i]
                if o["dma"] is None and o["fn"] is not None:
                    toks.append(("eng", e, i))
                    break
            n = self.ndma[e]
            for k in range(max(0, n - DMA_RING), n):
                toks.append(("dma", e, k))
        return toks

    def barrier(self):
        toks = self._tail_tokens()
        self.last_w = {}
        self.readers = {}
        for e in ENGS:
            deps = set(t for t in toks if not (t[0] == "eng" and t[1] == e and e in ("pe", "sp")))
            self.ops[e].append(dict(fn=None, deps=deps, dma=None, sig=False))

    def finish(self, eng="sp"):
        deps = set(t for t in self._tail_tokens() if t[0] == "dma")
        self.ops[eng].append(dict(fn=None, deps=deps, dma=None, sig=False))

    def emit(self):
        nc = self.nc
        for e in ENGS:
            for o in self.ops[e]:
                for t in o["deps"]:
                    if t[0] == "eng":
                        self.ops[t[1]][t[2]]["sig"] = True
        cnt = {}
        for e in ENGS:
            c = 0
            for i, o in enumerate(self.ops[e]):
                if o["sig"]:
                    assert o["fn"] is not None
                    c += 1
                cnt[(e, i)] = c
        with contextlib.ExitStack() as st:
            esem = {e: st.enter_context(nc.semaphore("prog_" + e)) for e in ENGS}
            dsem = {}
            for e in ENGS:
                if self.ndma[e]:
                    dsem[e] = [st.enter_context(nc.semaphore("dq_%s_%d" % (e, k)))
                               for k in range(min(DMA_RING, self.ndma[e]))]
            block = st.enter_context(nc.Block())
            ops = self.ops

            def run(e, eng):
                seen = {}
                for o in ops[e]:
                    need = {}
                    for t in o["deps"]:
                        if t[0] == "eng":
                            key = ("eng", t[1])
                            val = cnt[(t[1], t[2])]
                            sem = esem[t[1]]
                        else:
                            key = ("dma", t[1], t[2] % DMA_RING)
                            val = 16 * (t[2] // DMA_RING + 1)
                            sem = dsem[t[1]][t[2] % DMA_RING]
                        if seen.get(key, 0) >= val:
                            continue
                        if key not in need or need[key][1] < val:
                            need[key] = (sem, val)
                    for key, (sem, val) in need.items():
                        eng.wait_ge(sem, val)
                        seen[key] = val
                    if o["fn"] is None:
                        continue
                    ins = o["fn"](eng)
                    if o["dma"] is not None:
                        ins.then_inc(dsem[e][o["dma"] % DMA_RING], 16)
                    elif o["sig"]:
                        ins.then_inc(esem[e], 1)

            @block.tensor
            def _(eng):
                run("pe", eng)

            @block.scalar
            def _(eng):
                run("act", eng)

            @block.vector
            def _(eng):
                run("dve", eng)

            @block.gpsimd
            def _(eng):
                run("pool", eng)

            @block.sync
            def _(eng):
                run("sp", eng)


class Arena:
    def __init__(self, big):
        self.big = big
        self.cap = big.shape[1]
        self.top = 0

    def alloc(self, shape, dt):
        n = 1
        for s in shape[1:]:
            n *= s
        e16 = n * (2 if dt == F32 else 1)
        off = self.top
        self.top += (e16 + 15) // 16 * 16
        assert self.top <= self.cap, "SBUF arena overflow: %d > %d" % (self.top, self.cap)
        ap = self.big[0:shape[0], off:off + e16]
        if dt == F32:
            ap = ap.bitcast(F32)
        if len(shape) == 3:
            ap = ap.rearrange("p (a b) -> p a b", a=shape[1])
        elif len(shape) == 4:
            ap = ap.rearrange("p (a b c) -> p a b c", a=shape[1], b=shape[2])
        return ap


def build_nc(stop_after=None):
    nc = bass.Bass("TRN2", target_bir_lowering=False)

    def din(name, shape, dt=F32):
        return nc.dram_tensor(name, list(shape), dt, kind="ExternalInput").ap()

    x = din("x", [S_TOK, D])
    attn_norm = din("attn_norm", [D])
    w_qkv = din("w_qkv", [D, QKVW])
    w_attn_out = din("w_attn_out", [D, D])
    pool_norm = din("pool_norm", [D])
    w_pool_in = din("w_pool_in", [D, D])
    w_pool_group = din("w_pool_group", [4, 256, 256])
    pool_scale = din("pool_scale", [D])
    ffn_norm = din("ffn_norm", [2, D])
    w_gu = din("w_ffn_gate_up", [2, D, 2 * DFF])
    w_down = din("w_ffn_down", [2, DFF, D])
    final_norm = din("final_norm", [D])
    c_ident = din("c_ident", [128, 128], BF16)
    c_base = din("c_base", [128, 256])
    c_rc = din("c_rc", [4, 16])
    out = nc.dram_tensor("out", [S_TOK, D], F32, kind="ExternalOutput").ap()

    xs = [nc.dram_tensor("xs%d" % i, [S_TOK, D], F32).ap() for i in range(4)]
    OTs = nc.dram_tensor("ots", [D, S_TOK], BF16).ap()

    st = contextlib.ExitStack()
    with st:
        big = st.enter_context(nc.sbuf_tensor("big", [128, 105984], BF16))
        ps = [st.enter_context(nc.psum_tensor("ps%d" % i, [128, 512], F32)) for i in range(8)]
        S = Sched(nc)
        A = Arena(big)

        def tiles(ap):
            return ap.rearrange("(n p) d -> n p d", p=128)

        idt = A.alloc([128, 128], BF16)
        ones = A.alloc([128, 128], BF16)
        base2 = A.alloc([128, 512], F32)
        S.dma("sp", lambda e: e.dma_start(out=idt, in_=c_ident), w=["idt"])
        S.dma("sp", lambda e: e.dma_start(out=base2[:, 0:256], in_=c_base), w=["base"])
        S.dma("sp", lambda e: e.dma_start(out=base2[:, 256:512], in_=c_base), w=["base"])
        S.op("dve", lambda e: e.memset(ones, 1.0), w=["ones"])
        const_top = A.top

        def pT(b):
            return ps[b][:].bitcast(BF16)

        def sq_accum(xt_ap, junk, ssq_col, rx, wss):
            S.op("act", lambda e: e.activation(out=junk, in_=xt_ap, func=AF.Square, accum_out=ssq_col),
                 r=rx, w=["junk"] + wss)

        def rstd_from(ssq_ap, rst_ap, rr, ww):
            S.op("dve", lambda e: e.tensor_scalar(out=rst_ap, in0=ssq_ap, scalar1=1.0 / D, scalar2=EPS,
                                                 op0=ALU.mult, op1=ALU.add), r=rr, w=ww)
            S.op("act", lambda e: e.activation(out=rst_ap, in_=rst_ap, func=AF.Sqrt), r=ww, w=ww)
            S.op("dve", lambda e: e.reciprocal(out=rst_ap, in_=rst_ap), r=ww, w=ww)

        def normed_transposed(xt_ap, rst_col, gB, hn_ap, bank, dst_ap, rx, rrst, hn_res, dst_res):
            S.op("dve", lambda e: e.scalar_tensor_tensor(out=hn_ap, in0=xt_ap, scalar=rst_col, in1=gB,
                                                        op0=ALU.mult, op1=ALU.mult),
                 r=rx + rrst + ["gB"], w=[hn_res])

            def tr(e):
                for kc in range(8):
                    ins = e.transpose(out=pT(bank)[:, kc * 128:(kc + 1) * 128],
                                      in_=hn_ap[:, kc * 128:(kc + 1) * 128], identity=idt)
                return ins
            S.op("pe", tr, r=[hn_res, "idt"], w=["ps%d" % bank])
            S.op("act", lambda e: e.activation(out=dst_ap,
                                               in_=pT(bank).rearrange("p (a b) -> p a b", a=8),
                                               func=AF.Copy),
                 r=["ps%d" % bank], w=[dst_res])

        def phase_attention(xin, xout):
            hT = A.alloc([128, 8, S_TOK], BF16)
            p0_top = A.top
            gB = A.alloc([128, D], F32)
            ssq = A.alloc([128, 32], F32)
            rst = A.alloc([128, 32], F32)
            junk = A.alloc([128, D], BF16)
            xt = [A.alloc([128, D], F32) for _ in range(6)]
            hn = [A.alloc([128, D], BF16) for _ in range(2)]
            S.dma("sp", lambda e: e.dma_start(out=gB, in_=attn_norm.partition_broadcast(128)), w=["gB"])
            S.op("dve", lambda e: e.memset(ssq, 0.0), w=["ssq"])
            xin_t = tiles(xin)
            for tt in range(32):
                s3 = tt % 6
                S.dma("sp", lambda e, tt=tt, s3=s3: e.dma_start(out=xt[s3], in_=xin_t[tt]), w=["xt%d" % s3])
                sq_accum(xt[s3], junk, ssq[:, tt:tt + 1], ["xt%d" % s3, "ssq"], ["ssq"])
            rstd_from(ssq, rst, ["ssq"], ["rst"])
            for tt in range(32):
                s3, s2 = (tt + 32) % 6, tt % 2
                S.dma("sp", lambda e, tt=tt, s3=s3: e.dma_start(out=xt[s3], in_=xin_t[tt]), w=["xt%d" % s3])
                normed_transposed(xt[s3], rst[:, tt:tt + 1], gB, hn[s2], tt % 2,
                                  hT[:, :, tt * 128:(tt + 1) * 128],
                                  ["xt%d" % s3], ["rst"], "hn%d" % s2, "hT")
            S.barrier()
            A.top = p0_top
            XTs = [[A.alloc([128, S_TOK], BF16) for _ in range(3)] for _ in range(2)]
            Vtoks = [A.alloc([128, 32, 128], BF16) for _ in range(2)]
            PT = A.alloc([128, 32, 256], BF16)
            accU = A.alloc([128, S_TOK], F32)
            accD = A.alloc([128, S_TOK], F32)
            wq = [A.alloc([128, 8, 384], BF16) for _ in range(2)]
            OTh = A.alloc([128, S_TOK], BF16)
            Tb = [A.alloc([128, 512], F32) for _ in range(3)]
            XTn = ["QT", "KT", "VT"]

            iters = [(h, g) for h in range(8) for g in range(3)]

            def load_w(n):
                h, g = iters[n]
                slot = n % 2
                for part in range(3):
                    c0 = g * 3072 + part * 1024 + h * 128
                    S.dma("pool", lambda e, c0=c0, part=part, slot=slot: e.dma_start(
                        out=wq[slot][:, :, part * 128:(part + 1) * 128],
                        in_=w_qkv[:, c0:c0 + 128].rearrange("(kc p) c -> p kc c", p=128)),
                        w=["wq%d" % slot])

            evq = [0]

            def evac(dst, src, rr, ww):
                evq[0] += 1
                if evq[0] % 2:
                    S.op("act", lambda e: e.activation(out=dst, in_=src, func=AF.Copy), r=rr, w=ww)
                else:
                    S.op("dve", lambda e: e.tensor_copy(out=dst, in_=src), r=rr, w=ww)

            inv_sqrt_d = 1.0 / math.sqrt(128.0)
            def proj_units(n):
                h, g = iters[n]
                window, dil = PAIRS[g]
                slot = n % 2
                par = n % 2
                XT = XTs[par]
                Vtok = Vtoks[par]
                names = ["QT%d" % par, "KT%d" % par, "VT%d" % par]
                units = []
                for part in range(3):
                    for tc in range(8):
                        def unit(part=part, tc=tc):
                            bank = pbq[0] % 2
                            pbq[0] += 1

                            def mm(e):
                                for kc in range(8):
                                    ins = e.matmul(ps[bank][:], lhsT=wq[slot][:, kc, part * 128:(part + 1) * 128],
                                                   rhs=hT[:, kc, tc * 512:(tc + 1) * 512],
                                                   start=(kc == 0), stop=(kc == 7))
                                return ins
                            S.op("pe", mm, r=["wq%d" % slot, "hT"], w=["ps%d" % bank])
                            jn = 512 // dil
                            j0 = tc * jn
                            dst = XT[part].rearrange("p (r j) -> p r j", r=dil)[:, :, j0:j0 + jn]
                            src = ps[bank][:].rearrange("p (j r) -> p r j", r=dil)
                            evac(dst, src, ["ps%d" % bank], [names[part]])
                        units.append(unit)
                for q4 in range(4):
                    def unit(q4=q4):
                        bank = pbq[0] % 2
                        pbq[0] += 1

                        def trv(e):
                            for i in range(8):
                                b = q4 * 8 + i
                                ins = e.transpose(out=pT(bank)[:, i * 128:(i + 1) * 128],
                                                  in_=XT[2][:, b * 128:(b + 1) * 128], identity=idt)
                            return ins
                        S.op("pe", trv, r=[names[2], "idt"], w=["ps%d" % bank])
                        evac(Vtok[:, q4 * 8:(q4 + 1) * 8, :], pT(bank).rearrange("p (a b) -> p a b", a=8),
                             ["ps%d" % bank], ["Vtok%d" % par])
                    units.append(unit)
                return units

            pbq = [0]

            def attn(n, fillers):
                h, g = iters[n]
                window, dil = PAIRS[g]
                L = S_TOK // dil
                nb = L // 128
                par = n % 2
                XT = XTs[par]
                Vtok = Vtoks[par]
                QTn, KTn, VkN = "QT%d" % par, "KT%d" % par, "Vtok%d" % par
                slope = 2.0 ** (-8.0 * (g * 8 + h + 1) / 24.0)
                sc = slope * dil * math.sqrt(128.0)
                LAG = 5

                def s_pair(p):
                    b = 2 * p
                    last = ((b + 1) % nb) == nb - 1
                    W = 128 if last else 256
                    WW = 256 + W
                    bank = 2 + p % 4
                    tb = p % 3

                    def mms(e):
                        e.matmul(ps[bank][:, 0:256], lhsT=XT[1][:, b * 128:(b + 1) * 128],
                                 rhs=XT[0][:, b * 128:b * 128 + 256], start=True, stop=True)
                        return e.matmul(ps[bank][:, 256:256 + W], lhsT=XT[1][:, (b + 1) * 128:(b + 2) * 128],
                                        rhs=XT[0][:, (b + 1) * 128:(b + 1) * 128 + W], start=True, stop=True)
                    S.op("pe", mms, r=[QTn, KTn], w=["ps%d" % bank])
                    S.op("dve", lambda e: e.scalar_tensor_tensor(out=Tb[tb][:, :WW], in0=base2[:, :WW], scalar=sc,
                                                                in1=ps[bank][:, :WW],
                                                                op0=ALU.mult, op1=ALU.add),
                         r=["ps%d" % bank, "base"], w=["Tb%d" % tb])
                    S.op("act", lambda e: e.activation(
                        out=PT[:, b:b + 2, :].rearrange("p a b -> p (a b)")[:, :WW], in_=Tb[tb][:, :WW],
                        func=AF.Exp, scale=inv_sqrt_d),
                        r=["Tb%d" % tb], w=["PT%d" % b, "PT%d" % (b + 1)])

                def pv_group(b0, gi):
                    bank = 6 + gi % 2
                    rr = set([VkN, "ones"])

                    def mmpv(e):
                        ins = None
                        for which, lhs_of in ((0, lambda bb: Vtok[:, bb, :]), (1, lambda bb: ones)):
                            for i in range(2):
                                b = b0 + i
                                o = ps[bank][:, which * 256 + i * 128: which * 256 + (i + 1) * 128]
                                if b % nb > 0:
                                    e.matmul(o, lhsT=lhs_of(b - 1), rhs=PT[:, b - 1, 128:256], start=True, stop=False)
                                    ins = e.matmul(o, lhsT=lhs_of(b), rhs=PT[:, b, 0:128], start=False, stop=True)
                                else:
                                    ins = e.matmul(o, lhsT=lhs_of(b), rhs=PT[:, b, 0:128], start=True, stop=True)
                        return ins
                    for b in (b0 - 1, b0, b0 + 1):
                        if b >= 0:
                            rr.add("PT%d" % b)
                    S.op("pe", mmpv, r=sorted(rr), w=["psV%d" % (gi % 2)])
                    r_ = b0 // nb
                    n0 = b0 % nb
                    j0 = n0 * 128
                    for which, acc, nm in ((0, accU, "accU"), (1, accD, "accD")):
                        dst = acc.rearrange("p (j r) -> p r j", r=dil)[:, r_, j0:j0 + 256]
                        src = ps[bank][:, which * 256:(which + 1) * 256]
                        if g == 0:
                            S.op("act", lambda e, dst=dst, src=src: e.activation(out=dst, in_=src, func=AF.Copy),
                                 r=["psV%d" % (gi % 2)], w=[nm + "_g0", nm + "_g2"])
                        else:
                            S.op("dve", lambda e, dst=dst, src=src: e.tensor_tensor(out=dst, in0=src, in1=dst, op=ALU.add),
                                 r=["psV%d" % (gi % 2), nm + "_g%d" % (g - 1)], w=[nm + "_g%d" % g])

                LAGP = 2
                fi = 0
                for p in range(16 + LAGP):
                    if p < 16:
                        s_pair(p)
                    if p - LAGP >= 0:
                        pv_group(2 * (p - LAGP), p - LAGP)
                    for _ in range(2):
                        if fi < len(fillers):
                            fillers[fi]()
                            fi += 1
                while fi < len(fillers):
                    fillers[fi]()
                    fi += 1
                if g == 2:
                    S.op("dve", lambda e: e.reciprocal(out=accD, in_=accD), r=["accD_g2"], w=["accD_g2"])
                    S.op("dve", lambda e: e.tensor_tensor(out=OTh, in0=accU, in1=accD, op=ALU.mult),
                         r=["accU_g2", "accD_g2"], w=["OTh"])
                    S.dma("sp", lambda e, h=h: e.dma_start(out=OTs[h * 128:(h + 1) * 128, :], in_=OTh), r=["OTh"])
            iters = iters[:DBG['iters']]
            NI = len(iters)
            if NI > 0:
                load_w(0)
            if NI > 1:
                load_w(1)
            for u in (proj_units(0) if NI > 0 else []):
                u()
            for n_ in range(NI):
                if n_ + 2 < NI:
                    load_w(n_ + 2)
                attn(n_, proj_units(n_ + 1) if n_ + 1 < NI else [])
            S.barrier()
            A.top = const_top
            return

        def ffn_weights(layer, want_wd=True):
            Wgu = A.alloc([128, 8, 2 * DFF], BF16)
            wguv = w_gu[layer].rearrange("(kc p) c -> p kc c", p=128)
            for c in (0, 5, 6, 1, 7, 2, 8, 3, 9, 4, 10):
                S.dma("pool", lambda e, c=c: e.dma_start(out=Wgu[:, :, c * 512:(c + 1) * 512],
                                                        in_=wguv[:, :, c * 512:(c + 1) * 512]), w=["Wgu%d" % c])
            Wd = None
            if want_wd:
                Wd = load_wd(layer)
            return dict(Wgu=Wgu, Wd=Wd, top=A.top)

        def load_wd(layer):
            Wd = A.alloc([128, NF, D], BF16)
            wdv = w_down[layer].rearrange("(f p) c -> p f c", p=128)
            for hf in range(2):
                S.dma("pool", lambda e, hf=hf: e.dma_start(out=Wd[:, hf * 11:(hf + 1) * 11, :],
                                                          in_=wdv[:, hf * 11:(hf + 1) * 11, :]), w=["Wd%d" % hf])
            return Wd

        def phase_outproj(xin, xout, pre_layer=None):
            pre = ffn_weights(pre_layer) if pre_layer is not None else None
            Wo = A.alloc([128, 8, D], BF16)
            OTc = [A.alloc([128, 8, 512], BF16) for _ in range(2)]
            xr = [A.alloc([128, D], F32) for _ in range(4)]
            S.dma("pool", lambda e: e.dma_start(out=Wo, in_=w_attn_out.rearrange("(kc p) c -> p kc c", p=128)),
                  w=["Wo"])
            OTv = OTs.rearrange("(kc p) t -> p kc t", p=128)
            xin_t, xout_t = tiles(xin), tiles(xout)
            pb = 0
            for tc in range(8):
                cs = tc % 2
                S.dma("sp", lambda e, tc=tc, cs=cs: e.dma_start(out=OTc[cs], in_=OTv[:, :, tc * 512:(tc + 1) * 512]),
                      w=["OTc%d" % cs])
                for tt in range(4):
                    T = tc * 4 + tt
                    s2 = T % 4
                    S.dma("sp", lambda e, T=T, s2=s2: e.dma_start(out=xr[s2], in_=xin_t[T]), w=["xr%d" % s2])
                    for half in range(2):
                        bank = pb % 4
                        pb += 1

                        def mm(e, cs=cs, tt=tt, half=half, bank=bank):
                            for kc in range(8):
                                ins = e.matmul(ps[bank][:], lhsT=OTc[cs][:, kc, tt * 128:(tt + 1) * 128],
                                               rhs=Wo[:, kc, half * 512:(half + 1) * 512],
                                               start=(kc == 0), stop=(kc == 7))
                            return ins
                        S.op("pe", mm, r=["OTc%d" % cs, "Wo"], w=["ps%d" % bank])
                        S.op("dve", lambda e, s2=s2, half=half, bank=bank: e.tensor_tensor(
                            out=xr[s2][:, half * 512:(half + 1) * 512], in0=ps[bank][:],
                            in1=xr[s2][:, half * 512:(half + 1) * 512], op=ALU.add),
                            r=["ps%d" % bank, "xr%d" % s2], w=["xr%d" % s2])
                    S.dma("pool", lambda e, T=T, s2=s2: e.dma_start(out=xout_t[T], in_=xr[s2]), r=["xr%d" % s2])
            S.barrier()
            A.top = pre["top"] if pre else const_top
            return pre

        def phase_ffn(layer, xin, xout, pre=None):
            if pre is None:
                pre = ffn_weights(layer)
            Wgu = pre["Wgu"]
            Wd = pre["Wd"] if pre["Wd"] is not None else load_wd(layer)
            gB = A.alloc([128, D], F32)
            ssq = A.alloc([128, 32], F32)
            rst = A.alloc([128, 32], F32)
            junk = A.alloc([128, D], BF16)
            xn = [A.alloc([128, D], F32) for _ in range(4)]
            hn = [A.alloc([128, D], BF16) for _ in range(2)]
            h2T = A.alloc([128, 8, 512], BF16)
            actT = A.alloc([128, NF, 512], BF16)
            sg = [A.alloc([128, 512], F32) for _ in range(2)]
            xr = [A.alloc([128, D], F32) for _ in range(2)]
            S.dma("sp", lambda e: e.dma_start(out=gB, in_=ffn_norm[layer].partition_broadcast(128)), w=["gB"])
            S.op("dve", lambda e: e.memset(ssq, 0.0), w=["ssq%d" % c for c in range(8)])
            xin_t, xout_t = tiles(xin), tiles(xout)

            def norm_chunk(c):
                for tt in range(4):
                    T = c * 4 + tt
                    S.dma("sp", lambda e, T=T, tt=tt: e.dma_start(out=xn[tt], in_=xin_t[T]), w=["xn%d" % tt])
                    sq_accum(xn[tt], junk, ssq[:, T:T + 1], ["xn%d" % tt, "ssq%d" % c], ["ssq%d" % c])
                rstd_from(ssq[:, c * 4:(c + 1) * 4], rst[:, c * 4:(c + 1) * 4], ["ssq%d" % c], ["rst%d" % c])
                for tt in range(4):
                    T = c * 4 + tt
                    normed_transposed(xn[tt], rst[:, T:T + 1], gB, hn[tt % 2], tt % 2,
                                      h2T[:, :, tt * 128:(tt + 1) * 128],
                                      ["xn%d" % tt], ["rst%d" % c], "hn%d" % (tt % 2), "h2T")

            def gate_up(c):
                for f in range(NF):
                    gb, ub, s2 = 2 + f % 2, 4 + f % 2, f % 2
                    cg, cu = f // 4, (NF + f) // 4

                    def mm(e, f=f, gb=gb, ub=ub):
                        for bank, col in ((gb, f * 128), (ub, DFF + f * 128)):
                            for kc in range(8):
                                ins = e.matmul(ps[bank][:], lhsT=Wgu[:, kc, col:col + 128], rhs=h2T[:, kc, :],
                                               start=(kc == 0), stop=(kc == 7))
                        return ins
                    S.op("pe", mm, r=["Wgu%d" % cg, "Wgu%d" % cu, "h2T"], w=["ps%d" % gb, "ps%d" % ub])
                    S.op("act", lambda e, gb=gb, s2=s2: e.activation(out=sg[s2], in_=ps[gb][:], func=AF.Silu),
                         r=["ps%d" % gb], w=["sg%d" % s2])
                    S.op("dve", lambda e, f=f, ub=ub, s2=s2: e.tensor_tensor(out=actT[:, f, :], in0=ps[ub][:],
                                                                            in1=sg[s2], op=ALU.mult),
                         r=["ps%d" % ub, "sg%d" % s2], w=["actT"])

            def down(c):
                for tt in range(4):
                    T = c * 4 + tt
                    s2 = T % 2
                    S.dma("sp", lambda e, T=T, s2=s2: e.dma_start(out=xr[s2], in_=xin_t[T]), w=["xr%d" % s2])
                    for half in range(2):
                        bank = 6 + half

                        def mm(e, tt=tt, half=half, bank=bank):
                            for f in range(NF):
                                ins = e.matmul(ps[bank][:], lhsT=actT[:, f, tt * 128:(tt + 1) * 128],
                                               rhs=Wd[:, f, half * 512:(half + 1) * 512],
                                               start=(f == 0), stop=(f == NF - 1))
                            return ins
                        S.op("pe", mm, r=["actT", "Wd0", "Wd1"], w=["ps%d" % bank])
                        S.op("dve", lambda e, s2=s2, half=half, bank=bank: e.tensor_tensor(
                            out=xr[s2][:, half * 512:(half + 1) * 512], in0=ps[bank][:],
                            in1=xr[s2][:, half * 512:(half + 1) * 512], op=ALU.add),
                            r=["ps%d" % bank, "xr%d" % s2], w=["xr%d" % s2])
                    S.dma("pool", lambda e, T=T, s2=s2: e.dma_start(out=xout_t[T], in_=xr[s2]), r=["xr%d" % s2])

            norm_chunk(0)
            for c in range(8):
                gate_up(c)
                if c + 1 < 8:
                    norm_chunk(c + 1)
                down(c)
            S.barrier()
            A.top = const_top

        def phase_pool(xin, xout, pre_layer=None):
            pre = ffn_weights(pre_layer, want_wd=False) if pre_layer is not None else None
            Win = A.alloc([128, 8, D], BF16)
            Wg = A.alloc([128, 4, 2, 256], BF16)
            gB = A.alloc([128, D], F32)
            scB = A.alloc([128, D], F32)
            rc = A.alloc([128, 4, 16], F32)
            ssq = A.alloc([128, 32], F32)
            rst = A.alloc([128, 32], F32)
            junk = A.alloc([128, D], BF16)
            xn = [A.alloc([128, D], F32) for _ in range(4)]
            hn = [A.alloc([128, D], BF16) for _ in range(2)]
            hTc = [A.alloc([128, 8, 512], BF16) for _ in range(2)]
            uT = A.alloc([128, 8, 528], F32)
            chA = [A.alloc([128, 528], F32) for _ in range(2)]
            chB = [A.alloc([128, 528], F32) for _ in range(2)]
            yT = A.alloc([128, 8, 512], BF16)
            tmp = [A.alloc([128, 512], F32) for _ in range(2)]
            tmp16 = A.alloc([128, 16], F32)
            xr = [A.alloc([128, D], F32) for _ in range(3)]
            S.dma("pool", lambda e: e.dma_start(out=Win, in_=w_pool_in.rearrange("(kc p) c -> p kc c", p=128)),
                  w=["Win"])
            for gi in range(4):
                S.dma("pool", lambda e, gi=gi: e.dma_start(
                    out=Wg[:, gi, :, :], in_=w_pool_group[gi].rearrange("(kc p) d -> p kc d", p=128)), w=["Wg"])
            S.dma("sp", lambda e: e.dma_start(out=gB, in_=pool_norm.partition_broadcast(128)), w=["gB"])
            S.dma("sp", lambda e: e.dma_start(out=scB, in_=pool_scale.partition_broadcast(128)), w=["scB"])
            S.dma("sp", lambda e: e.dma_start(out=rc.rearrange("p a b -> p (a b)"),
                                              in_=c_rc.rearrange("a b -> (a b)").partition_broadcast(128)), w=["rc"])
            S.op("dve", lambda e: e.memset(ssq, 0.0), w=["ssq%d" % t for t in range(8)])
            S.op("dve", lambda e: e.memset(uT, 0.0), w=["uT%d" % ct for ct in range(8)])
            xin_t, xout_t = tiles(xin), tiles(xout)
            xq = [0]

            def norm_chunk(c):
                cs = c % 2
                for tt in range(4):
                    T = c * 4 + tt
                    S.dma("sp", lambda e, T=T, tt=tt: e.dma_start(out=xn[tt], in_=xin_t[T]), w=["xn%d" % tt])
                    sq_accum(xn[tt], junk, ssq[:, T:T + 1], ["xn%d" % tt, "ssq%d" % c], ["ssq%d" % c])
                rstd_from(ssq[:, c * 4:(c + 1) * 4], rst[:, c * 4:(c + 1) * 4], ["ssq%d" % c], ["rst%d" % c])
                for tt in range(4):
                    T = c * 4 + tt
                    normed_transposed(xn[tt], rst[:, T:T + 1], gB, hn[T % 2], T % 2,
                                      hTc[cs][:, :, tt * 128:(tt + 1) * 128],
                                      ["xn%d" % tt], ["rst%d" % c], "hn%d" % (T % 2), "hTc%d" % cs)

            def mix_chunk(c):
                cs = c % 2
                if c > 0:
                    S.op("dve", lambda e: e.tensor_copy(out=uT[:, :, 0:16], in_=uT[:, :, 512:528]),
                         r=["uT%d" % ct for ct in range(8)], w=["uT%d" % ct for ct in range(8)])
                for ct in range(8):
                    bank = 2 + ct % 2

                    def mm(e, ct=ct, bank=bank, cs=cs):
                        for kc in range(8):
                            ins = e.matmul(ps[bank][:], lhsT=Win[:, kc, ct * 128:(ct + 1) * 128], rhs=hTc[cs][:, kc, :],
                                           start=(kc == 0), stop=(kc == 7))
                        return ins
                    S.op("pe", mm, r=["Win", "hTc%d" % cs], w=["ps%d" % bank])
                    S.op("act", lambda e, ct=ct, bank=bank: e.activation(out=uT[:, ct, 16:528], in_=ps[bank][:],
                                                                       func=AF.Copy),
                         r=["ps%d" % bank], w=["uT%d" % ct])
                    gi = ct // 2
                    wdw = 2 ** (gi + 1)
                    cur = uT[:, ct, :]
                    cur_res = "uT%d" % ct
                    bufs = [chA[ct % 2], chB[ct % 2]]
                    bres = ["chA%d" % (ct % 2), "chB%d" % (ct % 2)]
                    for lev in range(gi + 1):
                        sh = 2 ** lev
                        lo = 2 * sh - 1
                        dstb = bufs[lev % 2]
                        S.op("dve" if gi >= 2 else "pool", lambda e, dstb=dstb, cur=cur, lo=lo, sh=sh: e.tensor_tensor(
                            out=dstb[:, lo:528], in0=cur[:, lo:528], in1=cur[:, lo - sh:528 - sh], op=ALU.add),
                            r=[cur_res], w=[bres[lev % 2]])
                        cur = dstb
                        cur_res = bres[lev % 2]
                    S.op("dve", lambda e, ct=ct, cur=cur, wdw=wdw: e.scalar_tensor_tensor(
                        out=yT[:, ct, :], in0=cur[:, 16:528], scalar=1.0 / wdw, in1=uT[:, ct, 16:528],
                        op0=ALU.mult, op1=ALU.subtract), r=[cur_res, "uT%d" % ct], w=["yT"])
                    if c == 0:
                        S.op("dve", lambda e, cur=cur, gi=gi: e.tensor_tensor(out=tmp16, in0=cur[:, 16:32],
                                                                            in1=rc[:, gi, :], op=ALU.mult),
                             r=[cur_res, "rc"], w=["tmp16"])
                        S.op("dve", lambda e, ct=ct: e.tensor_tensor(out=yT[:, ct, 0:16], in0=tmp16,
                                                                    in1=uT[:, ct, 16:32], op=ALU.subtract),
                             r=["tmp16", "uT%d" % ct, "yT"], w=["yT"])
                for tt in range(4):
                    T = c * 4 + tt
                    s2 = T % 3
                    S.dma("sp", lambda e, T=T, s2=s2: e.dma_start(out=xr[s2], in_=xin_t[T]), w=["xr%d" % s2])
                    for half in range(2):
                        bank = 4 + (T * 2 + half) % 4

                        def mm(e, tt=tt, half=half, bank=bank):
                            for gg in range(2):
                                gi = half * 2 + gg
                                for k2 in range(2):
                                    ins = e.matmul(ps[bank][:, gg * 256:(gg + 1) * 256],
                                                   lhsT=yT[:, gi * 2 + k2, tt * 128:(tt + 1) * 128],
                                                   rhs=Wg[:, gi, k2, :], start=(k2 == 0), stop=(k2 == 1))
                            return ins
                        S.op("pe", mm, r=["yT", "Wg"], w=["ps%d" % bank])
                        S.op("dve", lambda e, half=half, bank=bank: e.tensor_tensor(
                            out=tmp[half], in0=ps[bank][:], in1=scB[:, half * 512:(half + 1) * 512], op=ALU.mult),
                            r=["ps%d" % bank, "scB"], w=["tmp%d" % half])
                        S.op("pool", lambda e, s2=s2, half=half: e.tensor_tensor(
                            out=xr[s2][:, half * 512:(half + 1) * 512], in0=tmp[half],
                            in1=xr[s2][:, half * 512:(half + 1) * 512], op=ALU.add),
                            r=["tmp%d" % half, "xr%d" % s2], w=["xr%d" % s2])
                    S.dma("pool", lambda e, T=T, s2=s2: e.dma_start(out=xout_t[T], in_=xr[s2]), r=["xr%d" % s2])

            norm_chunk(0)
            for c in range(8):
                if c + 1 < 8:
                    norm_chunk(c + 1)
                mix_chunk(c)
            S.barrier()
            A.top = pre["top"] if pre else const_top
            return pre

        def phase_final(xin, xout):
            gB = A.alloc([128, D], F32)
            ssq = A.alloc([128, 32], F32)
            rst = A.alloc([128, 32], F32)
            junk = A.alloc([128, D], BF16)
            xn = [A.alloc([128, D], F32) for _ in range(8)]
            yo = [A.alloc([128, D], F32) for _ in range(6)]
            S.dma("sp", lambda e: e.dma_start(out=gB, in_=final_norm.partition_broadcast(128)), w=["gB"])
            S.op("dve", lambda e: e.memset(ssq, 0.0), w=["ssq"])
            xin_t, xout_t = tiles(xin), tiles(xout)
            for T in range(32):
                s4 = T % 8
                S.dma("sp", lambda e, T=T, s4=s4: e.dma_start(out=xn[s4], in_=xin_t[T]), w=["xn%d" % s4])
                sq_accum(xn[s4], junk, ssq[:, T:T + 1], ["xn%d" % s4, "ssq"], ["ssq"])
            rstd_from(ssq, rst, ["ssq"], ["rst"])
            for T in range(32):
                s4, s3 = T % 8, T % 6
                S.dma("sp", lambda e, T=T, s4=s4: e.dma_start(out=xn[s4], in_=xin_t[T]), w=["xn%d" % s4])
                S.op("dve", lambda e, T=T, s4=s4, s3=s3: e.scalar_tensor_tensor(
                    out=yo[s3], in0=xn[s4], scalar=rst[:, T:T + 1], in1=gB, op0=ALU.mult, op1=ALU.mult),
                    r=["xn%d" % s4, "rst", "gB"], w=["yo%d" % s3])
                S.dma("pool", lambda e, T=T, s3=s3: e.dma_start(out=xout_t[T], in_=yo[s3]), r=["yo%d" % s3])
            S.barrier()
            A.top = const_top

        def copy_out(src):
            for i in range(8):
                S.dma("sp", lambda e, i=i: e.dma_start(
                    out=out[i * 512:(i + 1) * 512, :].rearrange("(p a) d -> p (a d)", p=128),
                    in_=src[i * 512:(i + 1) * 512, :].rearrange("(p a) d -> p (a d)", p=128)))

        carry = {}
        stages = [
            ("attn", lambda: (phase_attention(x, xs[0]), carry.__setitem__("f0", phase_outproj(x, xs[0], pre_layer=0))), xs[0]),
            ("ffn0", lambda: phase_ffn(0, xs[0], xs[1], pre=carry.get("f0")), xs[1]),
            ("pool", lambda: carry.__setitem__("f1", phase_pool(xs[1], xs[2], pre_layer=1)), xs[2]),
            ("ffn1", lambda: phase_ffn(1, xs[2], xs[3], pre=carry.get("f1")), xs[3]),
            ("final", lambda: phase_final(xs[3], out), None),
        ]
        if DBG.get('nostage'):
            copy_out(x)
            stages = []
        for name, fn, res in stages:
            fn()
            if stop_after == name and res is not None:
                copy_out(res)
                break
        S.finish()
        S.emit()
    return nc


def _consts():
    ident = np.eye(128, dtype=np.float32).astype(ml_dtypes.bfloat16)
    k = np.arange(128)[:, None]
    q = np.arange(128)[None, :]
    cur = np.where(q - k >= 0, -(q - k).astype(np.float32), NEG)
    prev = np.where(k >= q, -(128 + q - k).astype(np.float32), NEG)
    base = np.concatenate([cur, prev], axis=1).astype(np.float32)
    rc = np.zeros((4, 16), np.float32)
    for gi, w in enumerate((2, 4, 8, 16)):
        rc[gi] = 1.0 / np.minimum(np.arange(1, 17), w)
    return ident, base, rc


_NC_CACHE = {}


def kernel(x, attn_norm, w_qkv, w_attn_out, pool_norm, w_pool_in, w_pool_group, pool_scale,
           ffn_norm, w_ffn_gate_up, w_ffn_down, final_norm, _stop_after=None):
    f = lambda a: np.ascontiguousarray(np.asarray(a), dtype=np.float32)
    x = f(x)
    ident, base, rc = _consts()
    shared = dict(
        attn_norm=f(attn_norm)[0], w_qkv=f(w_qkv)[0], w_attn_out=f(w_attn_out)[0],
        pool_norm=f(pool_norm)[0], w_pool_in=f(w_pool_in)[0], w_pool_group=f(w_pool_group)[0],
        pool_scale=f(pool_scale)[0], ffn_norm=f(ffn_norm), w_ffn_gate_up=f(w_ffn_gate_up),
        w_ffn_down=f(w_ffn_down), final_norm=f(final_norm),
        c_ident=ident, c_base=base, c_rc=rc,
    )
    if _stop_after not in _NC_CACHE:
        _NC_CACHE[_stop_after] = build_nc(_stop_after)
    nc = _NC_CACHE[_stop_after]
    ncores = DBG.get('ncores', 8)
    in_maps = [dict(shared, x=x[b]) for b in range(ncores)]
    res = run_bass_kernel_spmd(nc, in_maps, core_ids=list(range(ncores)))
    return np.stack([np.asarray(r["out"], dtype=np.float32) for r in res.results], axis=0)
```
